# Optimizing a Trainium2 kernel written in Bass

```python
import jax, jax.numpy as jnp
from jax import lax
import numpy as np

D_MODEL = 1024
BATCH = 8
SEQ = 2048
DEPTH = 4
DEC_BATCH = 128
DEC_SEQ = 8
PAST_LEN = 16384
PAGE_SIZE = 128

D_MIX = D_MODEL
D_POOL = D_MIX // 2
D_RET = D_MIX - D_POOL
POOL_WINDOWS = (2, 4, 8, 16)
POOL_GROUPS = len(POOL_WINDOWS)
POOL_CG = D_POOL // POOL_GROUPS
POOL_BUF = max(POOL_WINDOWS) - 1
RET_HEADS = 4
RET_DK = D_RET // RET_HEADS
RET_DV = D_RET // RET_HEADS
RET_CHUNK = 128
ROPE_BASE = 10000.0
D_IN = D_POOL + 3 * D_RET + D_MIX
EPS = 1e-6

kernel_name = 'hymba_pool_retention_decode_step'


def _rmsnorm(x, g):
    xf = x.astype(jnp.float32)
    y = xf * lax.rsqrt(jnp.mean(xf * xf, axis=-1, keepdims=True) + EPS)
    if g is not None:
        y = y * g.astype(jnp.float32)
    return y.astype(x.dtype)


def _rope(x, pos):
    half = x.shape[-1] // 2
    inv = 1.0 / (ROPE_BASE ** (jnp.arange(half, dtype=jnp.float32) / half))
    ang = pos.astype(jnp.float32)[:, None] * inv[None, :]
    cos = jnp.cos(ang)[None, :, None, :]
    sin = jnp.sin(ang)[None, :, None, :]
    xf = x.astype(jnp.float32)
    x1, x2 = xf[..., :half], xf[..., half:]
    return jnp.concatenate([x1 * cos - x2 * sin, x1 * sin + x2 * cos], axis=-1)


def _log_gamma():
    h = jnp.arange(RET_HEADS, dtype=jnp.float32)
    return jnp.log(1.0 - jnp.exp2(-5.0 - h))


def _pool_mixer(u_full, n_new, w_pool, scale):
    B, Lf, _ = u_full.shape
    uf = u_full.astype(jnp.float32).reshape(B, Lf, POOL_GROUPS, POOL_CG)
    P = jnp.concatenate([jnp.zeros((B, 1, POOL_GROUPS, POOL_CG), jnp.float32),
                         jnp.cumsum(uf, axis=1)], axis=1)
    t = jnp.arange(Lf - n_new, Lf)
    outs = []
    for g, w in enumerate(POOL_WINDOWS):
        lo = jnp.maximum(t + 1 - w, 0)
        s = P[:, t + 1, g] - P[:, lo, g]
        cnt = (t + 1 - lo).astype(jnp.float32)[None, :, None]
        outs.append(s / cnt - uf[:, t, g])
    m = jnp.stack(outs, axis=2)
    y = jnp.einsum('bngc,gcd->bngd', m, w_pool.astype(jnp.float32))
    y = y.reshape(B, n_new, D_POOL) * scale.astype(jnp.float32)
    return y


def _retention(q, k, v, S0):
    B, L, H, DK = q.shape
    DV = v.shape[-1]
    C = RET_CHUNK if L % RET_CHUNK == 0 else L
    n = L // C
    lg = _log_gamma()
    qc = q.reshape(B, n, C, H, DK)
    kc = k.reshape(B, n, C, H, DK)
    vc = v.astype(jnp.float32).reshape(B, n, C, H, DV)
    idx = jnp.arange(C)
    diff = idx[:, None] - idx[None, :]
    dmask = jnp.where((diff >= 0)[:, :, None],
                      jnp.exp(jnp.maximum(diff, 0).astype(jnp.float32)[:, :, None] * lg), 0.0)
    scores = jnp.einsum('bnihd,bnjhd->bnhij', qc, kc) * jnp.transpose(dmask, (2, 0, 1))
    intra = jnp.einsum('bnhij,bnjhe->bnihe', scores, vc)
    kdec = jnp.exp((C - 1 - idx).astype(jnp.float32)[:, None] * lg)
    kv = jnp.einsum('bnjhd,jh,bnjhe->bnhde', kc, kdec, vc)
    gC = jnp.exp(C * lg)

    def step(S, kv_c):
        return gC[None, :, None, None] * S + kv_c, S

    S_fin, S_prev = lax.scan(step, S0.astype(jnp.float32), jnp.moveaxis(kv, 1, 0))
    S_prev = jnp.moveaxis(S_prev, 0, 1)
    qdec = jnp.exp((idx + 1).astype(jnp.float32)[:, None] * lg)
    cross = jnp.einsum('bnihd,ih,bnhde->bnihe', qc, qdec, S_prev)
    o = (intra + cross).reshape(B, L, H, DV)
    return o, S_fin


def _layer(x, c, pos, S0, pool_buf, w_ada, b_ada, g_pre, g_post, w_in, w_pool, pool_scale, w_o):
    B, L, _ = x.shape
    mod = jax.nn.silu(c.astype(jnp.float32)) @ w_ada.astype(jnp.float32) + b_ada.astype(jnp.float32)
    shift, scl, gate_res = jnp.split(mod, 3, axis=-1)
    h = _rmsnorm(x, g_pre).astype(jnp.float32) * (1.0 + scl[:, None]) + shift[:, None]
    z = (h.astype(x.dtype) @ w_in).astype(jnp.float32)
    o1 = D_POOL
    o2 = o1 + D_RET
    o3 = o2 + D_RET
    o4 = o3 + D_RET
    u, q, k, v, gate = z[..., :o1], z[..., o1:o2], z[..., o2:o3], z[..., o3:o4], z[..., o4:]
    if pool_buf is None:
        u_full = u
    else:
        u_full = jnp.concatenate([pool_buf.astype(jnp.float32), u], axis=1)
    pool_out = _pool_mixer(u_full, L, w_pool, pool_scale)
    new_buf = u_full[:, -POOL_BUF:]
    q = _rope(q.reshape(B, L, RET_HEADS, RET_DK), pos)
    k = _rope(k.reshape(B, L, RET_HEADS, RET_DK), pos) * (RET_DK ** -0.5)
    v = v.reshape(B, L, RET_HEADS, RET_DV)
    o, S_new = _retention(q, k, v, S0)
    o = _rmsnorm(o, None).reshape(B, L, D_RET)
    mix = jnp.concatenate([pool_out, o], axis=-1) * jax.nn.silu(gate)
    y = mix.astype(x.dtype) @ w_o
    y = _rmsnorm(y, g_post).astype(jnp.float32)
    x_out = (x.astype(jnp.float32) + gate_res[:, None] * y).astype(x.dtype)
    return x_out, S_new, new_buf


def setup_inputs(seed: int = 0) -> dict:
    key = jax.random.key(seed)
    ks = jax.random.split(key, 16)
    f = jnp.float32
    x_prompt = jax.random.normal(ks[0], (BATCH, SEQ, D_MODEL), f)
    x_sample = jax.random.normal(ks[1], (DEC_BATCH, DEC_SEQ, D_MODEL), f)
    c_prompt = jax.random.normal(ks[2], (BATCH, D_MODEL), f)
    c_sample = jax.random.normal(ks[3], (DEC_BATCH, D_MODEL), f)
    state_ret = 0.1 * jax.random.normal(ks[4], (DEPTH, DEC_BATCH, RET_HEADS, RET_DK, RET_DV), f)
    state_pool = jax.random.normal(ks[5], (DEPTH, DEC_BATCH, POOL_BUF, D_POOL), f)
    w_ada = 0.5 * D_MODEL ** -0.5 * jax.random.normal(ks[6], (DEPTH, D_MODEL, 3 * D_MODEL), f)
    b_ada = 0.02 * jax.random.normal(ks[7], (DEPTH, 3 * D_MODEL), f)
    g_pre = 1.0 + 0.05 * jax.random.normal(ks[8], (DEPTH, D_MODEL), f)
    g_post = 1.0 + 0.05 * jax.random.normal(ks[9], (DEPTH, D_MODEL), f)
    w_in = D_MODEL ** -0.5 * jax.random.normal(ks[10], (DEPTH, D_MODEL, D_IN), f)
    w_pool = POOL_CG ** -0.5 * jax.random.normal(ks[11], (DEPTH, POOL_GROUPS, POOL_CG, POOL_CG), f)
    pool_scale = 1.0 + 0.1 * jax.random.normal(ks[12], (DEPTH, D_POOL), f)
    w_o = D_MIX ** -0.5 * jax.random.normal(ks[13], (DEPTH, D_MIX, D_MODEL), f)
    return {'x_prompt': x_prompt, 'x_sample': x_sample, 'c_prompt': c_prompt, 'c_sample': c_sample,
            'state_ret': state_ret, 'state_pool': state_pool, 'w_ada': w_ada, 'b_ada': b_ada,
            'g_pre': g_pre, 'g_post': g_post, 'w_in': w_in, 'w_pool': w_pool,
            'pool_scale': pool_scale, 'w_o': w_o}


def reference(x_prompt, x_sample, c_prompt, c_sample, state_ret, state_pool, w_ada, b_ada,
              g_pre, g_post, w_in, w_pool, pool_scale, w_o):
    pos_p = jnp.arange(x_prompt.shape[1])
    pos_s = PAST_LEN + jnp.arange(x_sample.shape[1])
    Bp = x_prompt.shape[0]
    hp, hs = x_prompt, x_sample
    ret_p, pool_p, ret_s, pool_s = [], [], [], []
    for l in range(DEPTH):
        S0 = jnp.zeros((Bp, RET_HEADS, RET_DK, RET_DV), jnp.float32)
        hp, Sp, bp = _layer(hp, c_prompt, pos_p, S0, None, w_ada[l], b_ada[l], g_pre[l], g_post[l],
                            w_in[l], w_pool[l], pool_scale[l], w_o[l])
        hs, Ss, bs = _layer(hs, c_sample, pos_s, state_ret[l], state_pool[l], w_ada[l], b_ada[l],
                            g_pre[l], g_post[l], w_in[l], w_pool[l], pool_scale[l], w_o[l])
        ret_p.append(Sp.astype(x_prompt.dtype))
        pool_p.append(bp.astype(x_prompt.dtype))
        ret_s.append(Ss.astype(state_ret.dtype))
        pool_s.append(bs.astype(state_pool.dtype))
    ret_prompt = jnp.stack(ret_p, axis=0)
    pool_prompt = jnp.stack(pool_p, axis=0)
    ret_sample = jnp.stack(ret_s, axis=0)
    pool_sample = jnp.stack(pool_s, axis=0)
    return (hp, hs, ret_prompt, pool_prompt, ret_sample, pool_sample)
```

```python
import numpy as np
from contextlib import ExitStack

import concourse.bass as bass
import concourse.mybir as mybir
from concourse.alu_op_type import AluOpType as ALU
from concourse.bass_utils import run_bass_kernel_spmd

F32 = mybir.dt.float32
BF = mybir.dt.bfloat16
AF = mybir.ActivationFunctionType

NL = 4
NT = 17
D = 1024
DIN = 3072
NB = 16
EPS = 1e-6
SMP_POS = 8
RING = 10
NJ = 8
WINDOWS = (2, 4, 8, 16)

KF_COS = 0
KF_SIN = KF_COS + 17 * 64
KF_DM = KF_SIN + 17 * 128
KF_QD = KF_DM + 1024
KF_KD = KF_QD + 1024
KF_ID = KF_KD + 8
KF_IND = KF_ID + 128
KF_E = KF_IND + 16
KF_N = KF_E + 128
KB_ID = 0
KB_BAND = 128
KB_IND = KB_BAND + 6 * 512
KB_N = KB_IND + 16


def _gammas():
    h = np.arange(4, dtype=np.float64)
    return 1.0 - np.exp2(-5.0 - h)


def make_consts():
    kf = np.zeros((128, KF_N), np.float32)
    kb = np.zeros((128, KB_N), np.float32)
    p = np.arange(128)
    half = 64
    inv = (1.0 / (np.float32(10000.0) ** (np.arange(half, dtype=np.float32) / np.float32(half)))).astype(np.float32)
    cos = np.zeros((128, 17, 64), np.float32)
    sin = np.zeros((128, 17, 2, 64), np.float32)
    for t in range(17):
        pos = (128 * t + p) if t < 16 else (16384 + (p % 8))
        ang = (pos.astype(np.float32)[:, None] * inv[None, :]).astype(np.float32)
        c = np.cos(ang.astype(np.float64)).astype(np.float32)
        s = np.sin(ang.astype(np.float64)).astype(np.float32)
        cos[:, t] = c
        sin[:, t, 0] = -s
        sin[:, t, 1] = s
    kf[:, KF_COS:KF_COS + 17 * 64] = cos.reshape(128, -1)
    kf[:, KF_SIN:KF_SIN + 17 * 128] = sin.reshape(128, -1)
    g = _gammas()
    scale = 128.0 ** -0.5
    dm = np.zeros((128, 2, 4, 128), np.float64)
    qd = np.zeros((128, 2, 4, 128), np.float64)
    kd = np.zeros((128, 2, 4), np.float64)
    i = np.arange(128)
    for h in range(4):
        diff = i[None, :] - i[:, None]
        dm[:, 0, h, :] = np.where(diff >= 0, scale * g[h] ** np.maximum(diff, 0), 0.0)
        sj = i[:, None] % 8
        si = i[None, :] % 8
        same = (i[:, None] // 8) == (i[None, :] // 8)
        dm[:, 1, h, :] = np.where(same & (si >= sj), scale * g[h] ** np.maximum(si - sj, 0), 0.0)
        qd[:, 0, h, :] = (g[h] ** (i + 1))[None, :]
        qd[:, 1, h, :] = (g[h] ** ((i % 8) + 1))[None, :]
        kd[:, 0, h] = scale * g[h] ** (127 - i)
        kd[:, 1, h] = scale * g[h] ** (7 - (i % 8))
    kf[:, KF_DM:KF_DM + 1024] = dm.reshape(128, -1)
    kf[:, KF_QD:KF_QD + 1024] = qd.reshape(128, -1)
    kf[:, KF_KD:KF_KD + 8] = kd.reshape(128, -1)
    kf[:, KF_ID:KF_ID + 128] = np.eye(128)
    kf[:, KF_IND:KF_IND + 16] = (p[:, None] // 8) == np.arange(16)[None, :]
    E = np.zeros((128, 128), np.float32)
    E[0, :] = 1.0
    for b in range(16):
        E[32 + b, 8 * b:8 * b + 8] = 1.0
    kf[:, KF_E:KF_E + 128] = E
    kb[:, KB_ID:KB_ID + 128] = np.eye(128)
    band = np.zeros((128, 6, 4, 128), np.float64)
    tp = np.arange(128)[:, None]
    tt = np.arange(128)[None, :]
    for gi, w in enumerate(WINDOWS):
        inwin = (tp <= tt) & (tp > tt - w)
        cnt = np.minimum(tt + 1, w)
        band[:, 0, gi, :] = np.where(inwin, 1.0 / cnt, 0.0) - (tp == tt)
        band[:, 1, gi, :] = np.where(inwin, 1.0 / w, 0.0) - (tp == tt)
        band[:, 2, gi, :] = np.where(tp > 128 + tt - w, 1.0 / w, 0.0)
        sp_, st_ = tp % 8, tt % 8
        same = (tp // 8) == (tt // 8)
        band[:, 3, gi, :] = np.where(same & (sp_ <= st_) & (sp_ > st_ - w), 1.0 / w, 0.0) - (tp == tt)
        for k in range(2):
            rb = tp // 15 + 8 * k
            rr = tp % 15
            ok = (tp < 120) & (rb == (tt // 8)) & (rr > 15 + st_ - w)
            band[:, 4 + k, gi, :] = np.where(ok, 1.0 / w, 0.0)
    kb[:, KB_BAND:KB_BAND + 6 * 512] = band.reshape(128, -1)
    kb[:, KB_IND:KB_IND + 16] = (p[:, None] // 8) == np.arange(16)[None, :]
    return kf, kb


class Trk:
    def __init__(self, nc, es):
        self.nc = nc
        self.es = es
        self.engs = {}
        for name, e in [("pe", nc.tensor), ("act", nc.scalar), ("dve", nc.vector),
                        ("pool", nc.gpsimd), ("sp", nc.sync)]:
            sem = es.enter_context(nc.semaphore("s_" + name))
            self.engs[name] = dict(e=e, sem=sem, cnt=0, seen={}, name=name)
        self.lastw = {}
        self.readers = {}
        self.dsems = {}
        self.nwaits = 0
        self.nops = {}
        self.step = None
        self.alt = None
        self.tags = {}

    def _wait(self, E, deps):
        for (sem, val) in deps:
            key = id(sem)
            if E["seen"].get(key, 0) < val:
                E["e"].wait_ge(sem, val)
                E["seen"][key] = val
                self.nwaits += 1

    def _deps(self, E, reads, writes):
        deps = []
        pe = E["name"] == "pe"
        for k in reads:
            w = self.lastw.get(k)
            if w is not None and not (pe and w[0] is E["sem"]):
                deps.append(w)
            if k in self.EXCL:
                for r in self.readers.get(k, ()):
                    if r[0] is not E["sem"]:
                        deps.append(r)
        for k in writes:
            for r in self.readers.get(k, ()):
                if pe and r[0] is E["sem"]:
                    continue
                deps.append(r)
            w = self.lastw.get(k)
            if w is not None and not (pe and w[0] is E["sem"]):
                deps.append(w)
        return deps

    def _record(self, tok, reads, writes):
        for k in reads:
            self.readers.setdefault(k, []).append(tok)
        for k in writes:
            self.lastw[k] = tok
            self.readers[k] = []

    EXCL = frozenset(["Y0", "Y1", "Z0", "Z1", "Z2", "TR", "PB", "QBK"])

    PERSTEP = frozenset(["qrot", "krot", "vbf0", "vbf1", "vdec0", "vdec1", "sg", "mixp", "mixr.0", "mixr.1", "mixr.2", "mixr.3",
                         "mixT", "qkT", "qdT", "PT", "mTb", "h0", "ubf0", "ubf1", "ubf2", "ubfs"] + ["hT%d.%d" % (i, c) for i in range(2) for c in range(8)])

    def _tagcheck(self, reads, writes):
        if self.step is None:
            return
        for k in reads:
            if k in self.PERSTEP and self.tags.get(k) is not None:
                assert self.tags[k] in (self.step, self.alt), ("stale read", k, self.tags[k], self.step, self.alt)
        for k in writes:
            if k in self.PERSTEP:
                self.tags[k] = self.step

    def op(self, eng, fn, reads=(), writes=()):
        E = self.engs[eng]
        self._tagcheck(reads, writes)
        self._wait(E, self._deps(E, reads, writes))
        ins = fn(E["e"])
        E["cnt"] += 1
        ins.then_inc(E["sem"], 1)
        self.nops[eng] = self.nops.get(eng, 0) + 1
        self._record((E["sem"], E["cnt"]), reads, writes)
        return ins

    def dma(self, eng, out, in_, slot, reads=(), writes=(), **kw):
        E = self.engs[eng]
        self._wait(E, self._deps(E, reads, writes))
        if slot not in self.dsems:
            self.dsems[slot] = [self.es.enter_context(self.nc.semaphore("d_" + slot)), 0]
        d = self.dsems[slot]
        ins = E["e"].dma_start(out=out, in_=in_, **kw)
        d[1] += 16
        ins.then_inc(d[0], 16)
        self._record((d[0], d[1]), reads, writes)
        return ins

    def finish(self, eng="sp"):
        E = self.engs[eng]
        for slot, d in self.dsems.items():
            if d[1] > 0:
                E["e"].wait_ge(d[0], d[1])


class _Stop(Exception):
    pass


def build_program(n_layers=NL, interleave=True, stop=0):
    nc = bass.Bass("TRN2", target_bir_lowering=False)
    dr = lambda n, s, k: nc.dram_tensor(n, s, F32, kind=k).ap()
    xin = dr("xin", [NT * 128, D], "ExternalInput")
    cc = dr("cc", [48, D], "ExternalInput")
    sret = dr("sret", [NL, NB, 4, 128, 128], "ExternalInput")
    spool = dr("spool", [NL, NB * 15, 512], "ExternalInput")
    w_ada = dr("w_ada", [NL, D, DIN], "ExternalInput")
    b_ada = dr("b_ada", [NL, DIN], "ExternalInput")
    g_pre = dr("g_pre", [NL, D], "ExternalInput")
    g_post = dr("g_post", [NL, D], "ExternalInput")
    w_in = dr("w_in", [NL, D, DIN], "ExternalInput")
    w_pool = dr("w_pool", [NL, 4, 128, 128], "ExternalInput")
    pscale = dr("pool_scale", [NL, 512], "ExternalInput")
    w_o = dr("w_o", [NL, D, D], "ExternalInput")
    kf_d = dr("kf", [128, KF_N], "ExternalInput")
    kb_d = dr("kb", [128, KB_N], "ExternalInput")
    y = dr("y", [NT * 128, D], "ExternalOutput")
    retp = dr("retp", [NL, 4, 128, 128], "ExternalOutput")
    poolp = dr("poolp", [NL, 15, 512], "ExternalOutput")
    rets = dr("rets", [NL, NB, 4, 128, 128], "ExternalOutput")
    pools = dr("pools", [NL, NB, 15, 512], "ExternalOutput")

    with ExitStack() as es:
        T = Trk(nc, es)
        sb = lambda n, s, d=F32: es.enter_context(nc.sbuf_tensor(n, s, d))
        ps = lambda n, s, d=F32: es.enter_context(nc.psum_tensor(n, s, d))

        Y = ps("Y", [128, 1024])
        Z = [ps("Z%d" % i, [128, 512]) for i in range(3)]
        TR = ps("TR", [128, 8, 128], BF)
        PB = ps("PB", [128, 512])
        QB_ = ps("QBK", [128, 512])
        Y0 = Y[:, 0:512]
        Y1 = Y[:, 512:1024]
        PBt = PB[:].bitcast(BF).rearrange("p (c i) -> p c i", c=8)

        kf = sb("kfs", [128, KF_N])
        kb = sb("kbs", [128, KB_N], BF)
        ring = [sb("ring%d" % i, [128, 8, 512], BF) for i in range(RING)]
        xb = [sb("xb%d" % i, [128, D]) for i in range(4)]
        silucT = sb("silucT", [128, 8, 48], BF)
        GT = [sb("GT%d" % i, [128, 8, 48]) for i in range(2)]
        ST = [sb("ST%d" % i, [128, 8, 48]) for i in range(2)]
        GPr = [sb("GProws%d" % i, [48, D]) for i in range(2)]
        GPp = sb("GPp", [128, D])
        wps = [sb("wps%d" % i, [128, 4, 128], BF) for i in range(2)]
        wpl = sb("wpl", [128, 4, 128])
        psc = sb("psc", [128, 512])
        S = sb("S", [128, 4, 128])
        Sbf = sb("Sbf", [128, 4, 128], BF)
        wst = sb("wst", [128, 8, 256], BF)
        bch = sb("bch", [48, 256])
        gch = sb("gch", [48, 256])
        m1 = sb("m1", [48, 256])
        m2 = sb("m2", [48, 256])
        h0 = sb("h0", [128, D], BF)
        hT = [sb("hT%d" % i, [128, 8, 128], BF) for i in range(2)]
        ubf = [sb("ubf%d" % i, [128, 512], BF) for i in range(3)]
        ubfs = sb("ubfs", [128, 512], BF)
        u32 = sb("u32", [128, 512])
        qrot = sb("qrot", [128, 512], BF)
        krot = sb("krot", [128, 512], BF)
        t2 = sb("t2", [128, 512])
        sqo = u32
        vbf2 = [sb("vbf%d" % i, [128, 512], BF) for i in range(2)]
        vdec2 = [sb("vdec%d" % i, [128, 512], BF) for i in range(2)]
        sg = sb("sg", [128, D])
        qkT = sb("qkT", [128, 8, 128], BF)
        qdT = sb("qdT", [128, 4, 128], BF)
        PT = sb("PT", [128, 4, 128], BF)
        mTb = sb("mTb", [128, 4, 128], BF)
        mix = sb("mix", [128, D], BF)
        mixT = sb("mixT", [128, 8, 128], BF)
        tmpA = sb("tmpA", [128, 128])
        stA = [sb("stA%d" % i, [128, 4]) for i in range(2)]
        stC = [sb("stC%d" % i, [128, 12]) for i in range(2)]
        stD = [sb("stD%d" % i, [128, 4]) for i in range(2)]
        mhalf = sb("mhalf", [128, 4])
        s0f = [sb("s0f%d" % i, [128, 4, 128]) for i in range(2)]
        s0b = [sb("s0b%d" % i, [128, 4, 128], BF) for i in range(2)]
        vblk = [sb("vblk%d" % i, [128, 4, 128], BF) for i in range(2)]
        QBs = [sb("QBs%d" % i, [128, 4, 128], BF) for i in range(4)]
        hist = sb("hist", [128, 2, 512], BF)
        wstf = wst[:].rearrange("p c n -> p (c n)").bitcast(F32)
        s0in = [(s0f[0][:], "s0f0"), (s0f[1][:], "s0f1"),
                (t2[:].rearrange("p (b e) -> p b e", b=4), "t2"),
                (u32[:].rearrange("p (b e) -> p b e", b=4), "u32")]
        s0out = [(wstf[:, 0:512].rearrange("p (b e) -> p b e", b=4), "s0f2"),
                 (wstf[:, 512:1024].rearrange("p (b e) -> p b e", b=4), "s0f3")]
        s0_state = {}

        kcos = kf[:, KF_COS:KF_COS + 17 * 64].rearrange("p (t d) -> p t d", t=17)
        ksin = kf[:, KF_SIN:KF_SIN + 17 * 128].rearrange("p (t a d) -> p t a d", t=17, a=2)
        kdm = kf[:, KF_DM:KF_DM + 1024].rearrange("p (s h i) -> p s h i", s=2, h=4)
        kqd = kf[:, KF_QD:KF_QD + 1024].rearrange("p (s h i) -> p s h i", s=2, h=4)
        kkd = kf[:, KF_KD:KF_KD + 8].rearrange("p (s h) -> p s h", s=2)
        identf = kf[:, KF_ID:KF_ID + 128]
        kind = kf[:, KF_IND:KF_IND + 16]
        kE = kf[:, KF_E:KF_E + 128]
        identb = kb[:, KB_ID:KB_ID + 128]
        kband = kb[:, KB_BAND:KB_BAND + 6 * 512].rearrange("p (v g t) -> p v g t", v=6, g=4)
        kindb = kb[:, KB_IND:KB_IND + 16]
        gam = _gammas()
        gC = [float(gam[h] ** 128) for h in range(4)]
        gC8 = [float(gam[h] ** 8) for h in range(4)]

        def rstd_ops(st, n, inv_n):
            T.op("dve", lambda e: e.tensor_scalar(out=st[:, n:2 * n], in0=st[:, 0:n], scalar1=inv_n, scalar2=EPS,
                                                  op0=ALU.mult, op1=ALU.add), reads=[st.name], writes=[st.name])
            T.op("pool", lambda e: e.tensor_tensor(out=st[:, 2 * n:3 * n], in0=st[:, n:2 * n], in1=mhalf[:, 0:n],
                                                   op=ALU.pow), reads=[st.name, "mhalf"], writes=[st.name])

        def ring_slot(l, j):
            return (NJ * l + j) % RING

        def load_wtile(l, j):
            slot = ring_slot(l, j)
            if j < 6:
                src = w_in[l].rearrange("(c p) n -> p c n", p=128)[:, :, 512 * j:512 * j + 512]
            else:
                src = w_o[l].rearrange("(c p) n -> p c n", p=128)[:, :, 512 * (j - 6):512 * (j - 6) + 512]
            T.dma("pool", ring[slot][:], src, "ring%d" % slot, writes=["ring%d" % slot])

        wl_state = {"next": 0}

        def pump_weights(max_new=2):
            issued = 0
            while wl_state["next"] < n_layers * NJ and issued < max_new:
                n = wl_state["next"]
                if wl_state.get("limit") is not None and n >= wl_state["limit"]:
                    break
                if n >= RING and not wl_done.get(n - RING, False):
                    break
                load_wtile(n // NJ, n % NJ)
                wl_state["next"] += 1
                issued += 1

        wl_done = {}

        def mark_wdone(l, j):
            wl_done[NJ * l + j] = True

        def mod_load(l, q, wload=True):
            T.step = None
            r = q // 4
            o = (q % 4) * 256
            c0 = 256 * q
            if wload:
                T.dma("pool", wst[:], w_ada[l].rearrange("(c p) n -> p c n", p=128)[:, :, c0:c0 + 256], "wst",
                      writes=["wst", "s0f2", "s0f3"])
            T.dma("sp", bch[:], b_ada[l:l + 1, c0:c0 + 256].broadcast_to([48, 256]), "bch", writes=["bch"])
            if r == 1:
                T.dma("sp", gch[:], g_pre[l:l + 1, o:o + 256].broadcast_to([48, 256]), "gch", writes=["gch"])
            elif r == 2:
                T.dma("sp", gch[:], g_post[l:l + 1, o:o + 256].broadcast_to([48, 256]), "gch", writes=["gch"])

        def mod_mm(l, q, wsrc=None, wkeys=("wst", "s0f2", "s0f3")):
            T.step = None
            par = l % 2
            r = q // 4
            o = (q % 4) * 256
            psM = QB_[0:48, 0:256]
            if wsrc is None:
                wsrc = wst
            for k in range(8):
                T.op("pe", lambda e: e.matmul(psM, lhsT=silucT[:, k, :], rhs=wsrc[:, k, :], start=(k == 0),
                                              stop=(k == 7)), reads=["silucT"] + list(wkeys), writes=["QBK"])
            T.op("dve", lambda e: e.tensor_tensor(out=m1[:], in0=psM, in1=bch[:], op=ALU.add),
                 reads=["QBK", "bch"], writes=["m1"])
            if r == 2:
                T.op("dve", lambda e: e.tensor_tensor(out=GPr[par][:, o:o + 256], in0=m1[:], in1=gch[:], op=ALU.mult),
                     reads=["m1", "gch"], writes=[GPr[par].name])
            elif r == 1:
                T.op("dve", lambda e: e.scalar_tensor_tensor(out=m2[:], in0=m1[:], scalar=1.0, in1=gch[:],
                                                             op0=ALU.add, op1=ALU.mult),
                     reads=["m1", "gch"], writes=["m2"])
            else:
                T.op("dve", lambda e: e.tensor_copy(out=m2[:], in_=m1[:]), reads=["m1"], writes=["m2"])

        def mod_tr(l, q):
            T.step = None
            par = l % 2
            r = q // 4
            if r == 2:
                return
            psMT = QB_[:, 256:352].rearrange("p (a b) -> p a b", a=2)
            for a in range(2):
                T.op("pe", lambda e: e.transpose(out=psMT[:, a, :], in_=m2[:, a * 128:(a + 1) * 128],
                                                 identity=identf[0:48, 0:48]),
                     reads=["m2", "kf"], writes=["QBK"])
            dst = (ST if r == 0 else GT)[par]
            cb = (q % 4) * 2
            T.op("act", lambda e: e.activation(out=dst[:, cb:cb + 2, :], in_=psMT, func=AF.Copy),
                 reads=["QBK"], writes=[dst.name])

        def mod_chunk(l, q, load=True):
            if load:
                mod_load(l, q)
            mod_mm(l, q)
            mod_tr(l, q)

        def build_gp(l, sample):
            dst = sg if sample else GPp
            if not sample:
                T.step = None
            gpr = GPr[l % 2]
            for n in range(2):
                if sample:
                    lhsT, rhs = kE[32:48, :], gpr[32:48, n * 512:(n + 1) * 512]
                else:
                    lhsT, rhs = kE[0:1, :], gpr[0:1, n * 512:(n + 1) * 512]
                T.op("pe", lambda e: e.matmul(QB_[:], lhsT=lhsT, rhs=rhs, start=True, stop=True),
                     reads=["kf", gpr.name], writes=["QBK"])
                T.op("act", lambda e: e.activation(out=dst[:, n * 512:(n + 1) * 512], in_=QB_[:], func=AF.Copy),
                     reads=["QBK"], writes=[dst.name])

        def layer_setup(l):
            T.step = None
            par = l % 2
            T.dma("sp", wpl[:], w_pool[l].rearrange("g c d -> c g d"), "wpl", writes=["wpl"])
            T.dma("sp", psc[:], pscale[l:l + 1, :].broadcast_to([128, 512]), "psc", writes=["psc"])
            T.op("dve", lambda e: e.tensor_tensor(out=wps[par][:], in0=wpl[:],
                                                  in1=psc[:].rearrange("p (g d) -> p g d", g=4), op=ALU.mult),
                 reads=["wpl", "psc"], writes=[wps[par].name])

        def load_hist(l):
            v = spool[l].rearrange("(k r) c -> r k c", k=2)
            T.dma("pool", hist[0:120, :, :], v, "hist", writes=["hist"])
            srcv = spool[l].rearrange("(b r) c -> b r c", r=15)[:, 8:15, :]
            T.dma("sp", pools[l][:, 0:7, :], srcv, "dd")

        def lt(s):
            l, pos = divmod(s, NT)
            if pos == SMP_POS:
                return l, 16
            return l, (pos if pos < SMP_POS else pos - 1)

        def xsrc(l, t):
            return (xin if l == 0 else y)[t * 128:(t + 1) * 128, :]

        def load_x(s):
            l, t = lt(s)
            k = s % 4
            T.dma("sp", xb[k][:], xsrc(l, t), "xl%d" % k, reads=[("yh", t)] if l > 0 else [],
                  writes=[xb[k].name])

        def stage_A1(s):
            T.step = s
            x = xb[s % 4]
            st = stA[s % 2]
            T.op("act", lambda e: e.activation(out=h0[:], in_=x[:], func=AF.Square, accum_out=st[:, 0:1]),
                 reads=[x.name], writes=["h0", st.name])
            rstd_ops(st, 1, 1.0 / D)

        def stage_A2(s):
            T.step = s
            x = xb[s % 4]
            st = stA[s % 2]
            T.op("act", lambda e: e.activation(out=h0[:], in_=x[:], func=AF.Copy, scale=st[:, 2:3]),
                 reads=[x.name, st.name], writes=["h0"])

        def stage_A_elem(s):
            stage_A1(s)
            stage_A2(s)

        def stage_A_pe(s, do_pe=True, do_ev=True):
            T.step = s
            l, t = lt(s)
            par = l % 2
            hTs = hT[s % 2]
            hk = [hTs.name + ".%d" % c for c in range(8)]
            if do_pe:
                for c in range(8):
                    T.op("pe", lambda e: e.transpose(out=PBt[:, c, :], in_=h0[:, c * 128:(c + 1) * 128],
                                                     identity=identb), reads=["h0", "kb"], writes=["PB"])
            if not do_ev:
                return
            if t < 16:
                for c in range(8):
                    T.op("dve", lambda e: e.tensor_scalar(out=hTs[:, c, :], in0=PBt[:, c, :],
                                                          scalar1=GT[par][:, c, 0:1], scalar2=ST[par][:, c, 0:1],
                                                          op0=ALU.mult, op1=ALU.add),
                         reads=["PB", GT[par].name, ST[par].name], writes=[hk[c]])
            else:
                for c in range(8):
                    gb = GT[par][:, c, 32:48].unsqueeze(2).broadcast_to([128, 16, 8])
                    sbb = ST[par][:, c, 32:48].unsqueeze(2).broadcast_to([128, 16, 8])
                    tv = tmpA[:].rearrange("p (b s) -> p b s", b=16)
                    T.op("dve", lambda e: e.tensor_tensor(out=tv, in0=PBt[:, c, :].rearrange("p (b s) -> p b s", b=16),
                                                          in1=gb, op=ALU.mult),
                         reads=["PB", GT[par].name], writes=["tmpA"])
                    T.op("dve", lambda e: e.tensor_tensor(out=hTs[:, c, :].rearrange("p (b s) -> p b s", b=16),
                                                          in0=tv, in1=sbb, op=ALU.add),
                         reads=["tmpA", ST[par].name], writes=[hk[c]])

        def stage_A(s):
            stage_A_elem(s)
            stage_A_pe(s)

        def rope(zb, t, dst):
            qv = zb[:].rearrange("p (h a d) -> p h a d", h=4, a=2)
            t2v = t2[:].rearrange("p (h a d) -> p h a d", h=4, a=2)
            dv = dst[:].rearrange("p (h a d) -> p h a d", h=4, a=2)
            cosB = kcos[:, t, :].unsqueeze(1).unsqueeze(1).broadcast_to([128, 4, 2, 64])
            for a in range(2):
                sinB = ksin[:, t, a, :].unsqueeze(1).broadcast_to([128, 4, 64])
                T.op("dve", lambda e: e.tensor_tensor(out=t2v[:, :, a, :], in0=qv[:, :, 1 - a, :], in1=sinB,
                                                      op=ALU.mult), reads=[zb.name, "kf"], writes=["t2"])
            T.op("dve", lambda e: e.tensor_tensor(out=qv, in0=qv, in1=cosB, op=ALU.mult),
                 reads=[zb.name, "kf"], writes=[zb.name])
            T.op("dve", lambda e: e.tensor_tensor(out=dv, in0=qv, in1=t2v, op=ALU.add),
                 reads=[zb.name, "t2"], writes=[dst.name])

        def stage_B(s):
            l, t = lt(s)
            smp = 1 if t == 16 else 0
            hTs = hT[s % 2]
            vbf, vdec = vbf2[s % 2], vdec2[s % 2]
            zb = [Z[0], Z[1], Z[0], Z[2], Z[1], Z[2]]

            def pe(j):
                T.step = s
                slot = ring_slot(l, j)
                for k in range(8):
                    T.op("pe", lambda e: e.matmul(zb[j][:], lhsT=hTs[:, k, :], rhs=ring[slot][:, k, :],
                                                  start=(k == 0), stop=(k == 7)),
                         reads=[hTs.name + ".%d" % k, "ring%d" % slot], writes=[zb[j].name])
                if t == 15:
                    mark_wdone(l, j)
                    if j < 2:
                        pump_weights(1)

            def ev(j):
                T.step = s
                if j == 0:
                    ub = ubfs if t == 16 else ubf[(16 * l + t) % 3]
                    T.op("act", lambda e: e.activation(out=ub[:], in_=Z[0][:], func=AF.Copy),
                         reads=["Z0"], writes=[ub.name])
                    if t >= 15:
                        T.op("act", lambda e: e.activation(out=u32[:], in_=Z[0][:], func=AF.Copy),
                             reads=["Z0"], writes=["u32"])
                        if t == 15:
                            T.dma("sp", poolp[l], u32[113:128, :], "u32o", reads=["u32"])
                        else:
                            T.dma("sp", pools[l][:, 7:15, :], u32[:], "u32o", reads=["u32"])
                elif j == 1:
                    rope(Z[1], t, qrot)
                elif j == 2:
                    rope(Z[0], t, krot)
                elif j == 3:
                    T.op("act", lambda e: e.activation(out=vbf[:], in_=Z[2][:], func=AF.Copy),
                         reads=["Z2"], writes=[vbf.name])
                    kdB = kkd[:, smp, :].unsqueeze(2).broadcast_to([128, 4, 128])
                    T.op("dve", lambda e: e.tensor_tensor(out=vdec[:].rearrange("p (h e) -> p h e", h=4),
                                                          in0=Z[2][:].rearrange("p (h e) -> p h e", h=4), in1=kdB,
                                                          op=ALU.mult), reads=["Z2", "kf"], writes=[vdec.name])
                elif j == 4:
                    T.op("act", lambda e: e.activation(out=sg[:, 0:512], in_=Z[1][:], func=AF.Silu),
                         reads=["Z1"], writes=["sg"])
                else:
                    T.op("act", lambda e: e.activation(out=sg[:, 512:1024], in_=Z[2][:], func=AF.Silu),
                         reads=["Z2"], writes=["sg"])

            ops = []
            for j in range(6):
                ops.append(lambda j=j: pe(j))
                ops.append(lambda j=j: ev(j))
            return ops

        def s0_load(l, p):
            h, bq = divmod(p, 4)
            buf, key = s0in[p % 4]
            T.dma("sp", buf, sret[l, 4 * bq:4 * bq + 4, h].rearrange("b d e -> d b e"), "s0l%d" % (p % 4),
                  writes=[key])

        def s0_prefetch(l, n=2):
            st_ = s0_state.setdefault(l, 0)
            for p in range(st_, n):
                s0_load(l, p)
            s0_state[l] = max(st_, n)

        smp_fill = {}

        def stage_C(s):
            T.step = s
            l, t = lt(s)
            par = l % 2
            smp = 1 if t == 16 else 0
            st = stC[s % 2]
            vbf, vdec = vbf2[s % 2], vdec2[s % 2]
            Qv = QB_[:].rearrange("p (h i) -> p h i", h=4)
            cross = (t > 0)
            for h in range(4):
                T.op("pe", lambda e: e.transpose(out=PBt[:, h, :], in_=qrot[:, h * 128:(h + 1) * 128], identity=identb),
                     reads=["qrot", "kb"], writes=["PB"])
            for h in range(4):
                T.op("pe", lambda e: e.transpose(out=PBt[:, 4 + h, :], in_=krot[:, h * 128:(h + 1) * 128],
                                                 identity=identb), reads=["krot", "kb"], writes=["PB"])
            T.op("act", lambda e: e.activation(out=qkT[:], in_=PBt, func=AF.Copy), reads=["PB"], writes=["qkT"])
            if cross:
                T.op("dve", lambda e: e.tensor_tensor(out=qdT[:], in0=PBt[:, 0:4, :], in1=kqd[:, smp], op=ALU.mult),
                     reads=["PB", "kf"], writes=["qdT"])
            yield
            T.step = s
            ucur = ubfs if t == 16 else ubf[(16 * l + t) % 3]
            uprev = ubf[(16 * l + t - 1) % 3]
            if 0 < t < 16:
                T.alt = l * NT + ((t - 1) if (t - 1) < SMP_POS else t)
            for g in range(4):
                gs = slice(g * 128, (g + 1) * 128)
                if t == 0:
                    T.op("pe", lambda e: e.matmul(PB[:, gs], lhsT=ucur[:, gs], rhs=kband[:, 0, g, :], start=True,
                                                  stop=True), reads=[ucur.name, "kb"], writes=["PB"])
                elif t < 16:
                    T.op("pe", lambda e: e.matmul(PB[:, gs], lhsT=ucur[:, gs], rhs=kband[:, 1, g, :], start=True,
                                                  stop=False), reads=[ucur.name, "kb"], writes=["PB"])
                    T.op("pe", lambda e: e.matmul(PB[:, gs], lhsT=uprev[:, gs], rhs=kband[:, 2, g, :], start=False,
                                                  stop=True), reads=[uprev.name, "kb"], writes=["PB"])
                else:
                    T.op("pe", lambda e: e.matmul(PB[:, gs], lhsT=ucur[:, gs], rhs=kband[:, 3, g, :], start=True,
                                                  stop=False), reads=[ucur.name, "kb"], writes=["PB"])
                    for k in range(2):
                        T.op("pe", lambda e: e.matmul(PB[:, gs], lhsT=hist[0:120, k, gs], rhs=kband[0:120, 4 + k, g, :],
                                                      start=False, stop=(k == 1)),
                             reads=["hist", "kb"], writes=["PB"])
            T.alt = None
            T.op("act", lambda e: e.activation(out=mTb[:].rearrange("p g t -> p (g t)"), in_=PB[:], func=AF.Copy),
                 reads=["PB"], writes=["mTb"])
            for h in range(4):
                T.op("pe", lambda e: e.matmul(Qv[:, h, :], lhsT=qkT[:, 4 + h, :], rhs=qkT[:, h, :], start=True,
                                              stop=True), reads=["qkT"], writes=["QBK"])
            T.op("dve", lambda e: e.tensor_tensor(out=PT[:], in0=Qv, in1=kdm[:, smp], op=ALU.mult),
                 reads=["QBK", "kf"], writes=["PT"])
            if not smp:
                for h in range(4):
                    hs = slice(h * 128, (h + 1) * 128)
                    T.op("pe", lambda e: e.matmul(Y1[:, hs], lhsT=krot[:, hs], rhs=vdec[:, hs], start=True, stop=True),
                         reads=["krot", vdec.name], writes=["Y1"])
            yield
            T.step = s
            YP, YPn = (Y1, "Y1") if smp else (Y0, "Y0")
            for g in range(4):
                gs = slice(g * 128, (g + 1) * 128)
                T.op("pe", lambda e: e.matmul(YP[:, gs], lhsT=mTb[:, g, :], rhs=wps[par][:, g, :], start=True,
                                              stop=True), reads=["mTb", wps[par].name], writes=[YPn])
            if not smp:
                for h in range(4):
                    hs = slice(h * 128, (h + 1) * 128)
                    T.op("pe", lambda e: e.matmul(Qv[:, h, :], lhsT=PT[:, h, :], rhs=vbf[:, hs], start=True,
                                                  stop=not cross), reads=["PT", vbf.name], writes=["QBK"])
                    if cross:
                        T.op("pe", lambda e: e.matmul(Qv[:, h, :], lhsT=qdT[:, h, :], rhs=Sbf[:, h, :], start=False,
                                                      stop=True), reads=["qdT", "Sbf"], writes=["QBK"])
            else:
                s0_prefetch(l, 4)

                def build_vblk(p_):
                    h_, bq_ = divmod(p_, 4)
                    vsrc = vdec[:, h_ * 128:(h_ + 1) * 128].unsqueeze(1).broadcast_to([128, 4, 128])
                    isrc = kindb[:, 4 * bq_:4 * bq_ + 4].unsqueeze(2).broadcast_to([128, 4, 128])
                    vb_ = vblk[p_ % 2]
                    T.op("dve", lambda e: e.tensor_tensor(out=vb_[:], in0=vsrc, in1=isrc, op=ALU.mult),
                         reads=[vdec.name, "kb"], writes=[vb_.name])

                for p in range(16):
                    h, bq = divmod(p, 4)
                    hs = slice(h * 128, (h + 1) * 128)
                    if p in smp_fill:
                        smp_fill.pop(p)()
                        T.step = s
                    if bq == 0:
                        T.op("pe", lambda e: e.matmul(Qv[:, h, :], lhsT=PT[:, h, :], rhs=vbf[:, hs], start=True,
                                                      stop=False), reads=["PT", vbf.name], writes=["QBK"])
                    k2 = p % 2
                    f, fk = s0in[p % 4]
                    o, ok = s0out[k2]
                    b_, vb = s0b[k2], vblk[k2]
                    T.op("act", lambda e: e.activation(out=b_[:], in_=f, func=AF.Copy), reads=[fk], writes=[b_.name])
                    qsrc = qdT[:, h, 32 * bq:32 * bq + 32].rearrange("p (b c) -> p b c", b=4)
                    qfl = QBs[bq][:].rearrange("p b i -> p (b i)")
                    qdst = bass.AP(tensor=qfl.tensor, offset=qfl.offset + 32 * bq,
                                   ap=[list(qfl.ap[0]), [136, 4], [1, 8]])
                    T.op("pool", lambda e: e.tensor_copy(out=qdst, in_=qsrc), reads=["qdT"],
                         writes=[QBs[bq].name])
                    for b in range(4):
                        T.op("pe", lambda e: e.matmul(Qv[:, h, :], lhsT=QBs[bq][:, b, :], rhs=b_[:, b, :],
                                                      start=False, stop=(bq == 3 and b == 3)),
                             reads=[QBs[bq].name, b_.name], writes=["QBK"])
                    if p == 0:
                        build_vblk(0)
                    if p + 1 < 16:
                        build_vblk(p + 1)
                    yb = Y0 if k2 == 0 else PB[:]
                    ybn = "Y0" if k2 == 0 else "PB"
                    T.op("pe", lambda e: e.matmul(yb, lhsT=krot[:, hs], rhs=vb[:].rearrange("p b e -> p (b e)"),
                                                  start=True, stop=True), reads=["krot", vb.name], writes=[ybn])
                    T.op("dve", lambda e: e.scalar_tensor_tensor(out=o.rearrange("p b e -> p (b e)"),
                                                                 in0=f.rearrange("p b e -> p (b e)"),
                                                                 scalar=gC8[h], in1=yb, op0=ALU.mult, op1=ALU.add),
                         reads=[fk, ybn], writes=[ok])
                    T.dma("sp", rets[l, 4 * bq:4 * bq + 4, h].rearrange("b d e -> d b e"), o, "s0o%d" % k2,
                          reads=[ok])
                    if p + 4 < 16:
                        s0_load(l, p + 4)
            yield
            T.step = s
            MK = ["mixp"] + ["mixr.%d" % h for h in range(4)]
            T.op("act", lambda e: e.activation(out=sqo[:], in_=QB_[:], func=AF.Square), reads=["QBK"], writes=["u32"])
            T.op("dve", lambda e: e.tensor_reduce(out=st[:, 0:4], in_=sqo[:].rearrange("p (h e) -> p h e", h=4),
                                                  axis=mybir.AxisListType.X, op=ALU.add),
                 reads=["u32"], writes=[st.name])
            rstd_ops(st, 4, 1.0 / 128)
            T.op("dve", lambda e: e.tensor_tensor(out=mix[:, 0:512], in0=YP, in1=sg[:, 0:512], op=ALU.mult),
                 reads=[YPn, "sg"], writes=["mixp"])
            for h in range(4):
                cs = slice(512 + h * 128, 512 + (h + 1) * 128)
                T.op("dve", lambda e: e.scalar_tensor_tensor(out=mix[:, cs], in0=Qv[:, h, :], scalar=st[:, 8 + h:9 + h],
                                                             in1=sg[:, cs], op0=ALU.mult, op1=ALU.mult),
                     reads=["QBK", st.name, "sg"], writes=[MK[1 + h]])
            yield
            T.step = s
            if not smp:
                SK = ["S.%d" % h for h in range(4)]
                if t == 0:
                    T.op("dve", lambda e: e.tensor_copy(out=S[:].rearrange("p h e -> p (h e)"), in_=Y1),
                         reads=["Y1"], writes=SK)
                else:
                    for h in range(4):
                        hs = slice(h * 128, (h + 1) * 128)
                        T.op("dve", lambda e: e.scalar_tensor_tensor(out=S[:, h, :], in0=S[:, h, :], scalar=gC[h],
                                                                     in1=Y1[:, hs], op0=ALU.mult, op1=ALU.add),
                             reads=[SK[h], "Y1"], writes=[SK[h]])
                if t == 15:
                    T.dma("sp", retp[l].rearrange("h d e -> d h e"), S[:], "So", reads=SK)
            yield
            T.step = s
            for c in range(8):
                T.op("pe", lambda e: e.transpose(out=TR[:, c, :], in_=mix[:, c * 128:(c + 1) * 128], identity=identb),
                     reads=[MK[0] if c < 4 else MK[c - 3], "kb"], writes=["TR"])
            T.op("act", lambda e: e.activation(out=mixT[:], in_=TR[:], func=AF.Copy), reads=["TR"], writes=["mixT"])
            yield
            T.step = s

        def sbf_cast(s):
            l, t = lt(s)
            if t < 15:
                T.step = None
                T.op("pool", lambda e: e.tensor_copy(out=Sbf[:], in_=S[:]), reads=["S.%d" % h for h in range(4)],
                     writes=["Sbf"])

        def stage_D(s):
            T.step = s
            l, t = lt(s)
            x = xb[s % 4]
            st = stD[s % 2]
            if t == 16:
                build_gp(l, True)
            gp = sg if t == 16 else GPp
            for n in range(2):
                slot = ring_slot(l, 6 + n)
                yn = Y0 if n == 0 else Y1
                for k in range(8):
                    T.op("pe", lambda e: e.matmul(yn, lhsT=mixT[:, k, :], rhs=ring[slot][:, k, :], start=(k == 0),
                                                  stop=(k == 7)), reads=["mixT", "ring%d" % slot],
                         writes=["Y%d" % n])
                if t == 15:
                    mark_wdone(l, 6 + n)
            T.op("act", lambda e: e.activation(out=mix[:], in_=Y[:], func=AF.Square, accum_out=st[:, 0:1]),
                 reads=["Y0", "Y1"], writes=["mixp", "mixr.0", "mixr.1", "mixr.2", "mixr.3", st.name])
            T.op("dve", lambda e: e.tensor_tensor(out=Y[:], in0=Y[:], in1=gp[:], op=ALU.mult),
                 reads=["Y0", "Y1", gp.name], writes=["Y0", "Y1"])
            rstd_ops(st, 1, 1.0 / D)
            T.op("dve", lambda e: e.scalar_tensor_tensor(out=x[:], in0=Y[:], scalar=st[:, 2:3], in1=x[:],
                                                         op0=ALU.mult, op1=ALU.add),
                 reads=["Y0", "Y1", st.name, x.name], writes=[x.name])
            T.dma("sp", y[t * 128:(t + 1) * 128, :], x[:], "xs%d" % (s % 4), reads=[x.name], writes=[("yh", t)])

        def ck(n):
            if stop == n:
                raise _Stop()

        load_pos = [1, 2, 3, 4, 5, 9, 10, 11, 12, 13, 14, 15]
        MOD_LOAD_AT = {p_: q_ for q_, p_ in enumerate(load_pos)}
        MOD_MM_AT = {p_ + 1: q_ for q_, p_ in enumerate(load_pos)}
        MOD_TR_AT = {p_ + 2: q_ for q_, p_ in enumerate(load_pos) if q_ < 8}

        try:
            T.dma("sp", kf[:], kf_d, "kf", writes=["kf"])
            for c0 in range(0, KB_N, 1600):
                c1 = min(KB_N, c0 + 1600)
                T.dma("pool", kb[:, c0:c1], kb_d[:, c0:c1], "kb", writes=["kb"])
            T.op("dve", lambda e: e.memset(mhalf[:], -0.5), writes=["mhalf"])
            for i in range(4):
                T.op("pool", lambda e: e.memset(QBs[i][:], 0.0), writes=[QBs[i].name])
            T.op("pool", lambda e: e.memset(hist[:], 0.0), writes=["hist"])
            ck(1)
            T.dma("sp", sg[0:48, :], cc, "cc", writes=["sg"])
            T.op("act", lambda e: e.activation(out=xb[3][0:48, :], in_=sg[0:48, :], func=AF.Silu),
                 reads=["sg"], writes=[xb[3].name])
            cT = QB_[:, 0:384].rearrange("p (c r) -> p c r", c=8)
            for c in range(8):
                T.op("pe", lambda e: e.transpose(out=cT[:, c, :], in_=xb[3][0:48, c * 128:(c + 1) * 128],
                                                 identity=identf[0:48, 0:48]), reads=[xb[3].name, "kf"], writes=["QBK"])
            T.op("act", lambda e: e.activation(out=silucT[:], in_=cT, func=AF.Copy), reads=["QBK"], writes=["silucT"])
            ck(2)
            nsteps = n_layers * NT
            load_x(0)
            load_x(1)
            wl_state["limit"] = 8
            w_ada0 = w_ada[0].rearrange("(c p) n -> p c n", p=128)
            for qq in range(6):
                slot = 8 + qq % 2
                T.dma("pool", ring[slot][:], w_ada0[:, :, 512 * qq:512 * qq + 512], "ring%d" % slot,
                      writes=["ring%d" % slot])
                if qq == 1:
                    pump_weights(99)
                if qq >= 1:
                    for half in range(2):
                        q = 2 * (qq - 1) + half
                        mod_load(0, q, wload=False)
                        mod_mm(0, q, wsrc=ring[8 + (qq - 1) % 2][:, :, 256 * half:256 * half + 256],
                               wkeys=("ring%d" % (8 + (qq - 1) % 2),))
                        mod_tr(0, q)
            for half in range(2):
                q = 10 + half
                mod_load(0, q, wload=False)
                mod_mm(0, q, wsrc=ring[9][:, :, 256 * half:256 * half + 256], wkeys=("ring9",))
                mod_tr(0, q)
            wl_state["limit"] = None
            ck(3)
            pump_weights(99)
            layer_setup(0)
            load_hist(0)
            load_x(2)
            build_gp(0, False)
            ck(4)
            stage_A(0)
            if nsteps > 1:
                stage_A(1)
            ck(5)
            for f_ in stage_B(0):
                f_()
            ck(6)

            for s in range(nsteps):
                l, t = lt(s)
                pos = s % NT
                if pos == 0 and l > 0:
                    build_gp(l, False)
                gC_ = stage_C(s)
                Bops = stage_B(s + 1) if s + 1 < nsteps else [lambda: None] * 12
                U_PE, U_EV, Q_PE, Q_EV, K_PE, K_EV, V_PE, V_EV, G0_PE, G0_EV, G1_PE, G1_EV = Bops
                ncs = lambda: next(gC_)
                hasA = s + 2 < nsteps
                if interleave:
                    if s + 3 < nsteps:
                        load_x(s + 3)
                    if t == 16:
                        s0_prefetch(l)
                    ncs()
                    if hasA:
                        stage_A1(s + 2)
                    U_PE(); U_EV(); Q_PE(); Q_EV()
                    ncs()
                    if hasA:
                        stage_A2(s + 2)
                    pump_weights(1)
                    if t == 16:
                        smp_fill.clear()
                        smp_fill.update({2: K_PE, 6: lambda: (V_PE(), V_EV()), 10: G0_PE, 14: G1_PE})
                        ncs()
                        if hasA:
                            stage_A_pe(s + 2, do_ev=False)
                        ncs()
                        ncs()
                        if hasA:
                            stage_A_pe(s + 2, do_pe=False)
                        K_EV()
                        ncs()
                        stage_D(s)
                        G0_EV(); G1_EV()
                    else:
                        K_PE()
                        ncs()
                        if hasA:
                            stage_A_pe(s + 2, do_ev=False)
                        V_PE()
                        ncs()
                        V_EV()
                        ncs()
                        sbf_cast(s)
                        if hasA:
                            stage_A_pe(s + 2, do_pe=False)
                        K_EV()
                        G0_PE(); G0_EV()
                        ncs()
                        pump_weights(1)
                        if l + 1 < n_layers and pos in MOD_TR_AT:
                            mod_tr(l + 1, MOD_TR_AT[pos])
                        G1_PE(); G1_EV()
                        stage_D(s)
                else:
                    for ci, _ in enumerate(gC_):
                        ck(1000 + 10 * s + ci)
                    ck(1900 + s)
                    stage_D(s)
                    ck(2000 + s)
                    for f_ in Bops:
                        f_()
                    sbf_cast(s)
                    ck(3000 + s)
                pump_weights()
                if not interleave and s + 3 < nsteps:
                    load_x(s + 3)
                if not interleave and s + 2 < nsteps:
                    stage_A(s + 2)
                if l + 1 < n_layers:
                    if pos in MOD_TR_AT and not interleave:
                        mod_tr(l + 1, MOD_TR_AT[pos])
                    if pos in MOD_MM_AT:
                        mod_mm(l + 1, MOD_MM_AT[pos])
                    if pos in MOD_LOAD_AT:
                        mod_load(l + 1, MOD_LOAD_AT[pos])
                    if pos == 14:
                        layer_setup(l + 1)
                    if pos == 15:
                        load_hist(l + 1)
                if pos == SMP_POS - 1:
                    s0_prefetch(l)
                ck(4000 + s)
        except _Stop:
            pass
        T.finish()
        build_program.stats = dict(nops=dict(T.nops), nwaits=T.nwaits)
    return nc


_CACHE = {}


def kernel(x_prompt, x_sample, c_prompt, c_sample, state_ret, state_pool, w_ada, b_ada,
           g_pre, g_post, w_in, w_pool, pool_scale, w_o):
    f = lambda a: np.ascontiguousarray(np.asarray(a, dtype=np.float32))
    x_prompt, x_sample, c_prompt, c_sample = f(x_prompt), f(x_sample), f(c_prompt), f(c_sample)
    state_ret, state_pool = f(state_ret), f(state_pool)
    shared = dict(w_ada=f(w_ada), b_ada=f(b_ada), g_pre=f(g_pre), g_post=f(g_post), w_in=f(w_in),
                  w_pool=f(w_pool), pool_scale=f(pool_scale), w_o=f(w_o))
    kf, kb = make_consts()
    shared["kf"] = kf
    shared["kb"] = kb
    if "nc" not in _CACHE:
        _CACHE["nc"] = build_program()
    nc = _CACHE["nc"]
    in_maps = []
    for i in range(8):
        xs = x_sample[NB * i:NB * (i + 1)].reshape(NB * 8, D)
        cc = np.zeros((48, D), np.float32)
        cc[0] = c_prompt[i]
        cc[32:48] = c_sample[NB * i:NB * (i + 1)]
        m = dict(shared)
        m["xin"] = np.ascontiguousarray(np.concatenate([x_prompt[i], xs], axis=0))
        m["cc"] = cc
        m["sret"] = np.ascontiguousarray(state_ret[:, NB * i:NB * (i + 1)])
        m["spool"] = np.ascontiguousarray(state_pool[:, NB * i:NB * (i + 1)].reshape(NL, NB * 15, 512))
        in_maps.append(m)
    res = run_bass_kernel_spmd(nc, in_maps, core_ids=list(range(8)))
    outs = res.results
    y_p = np.stack([outs[i]["y"][:2048] for i in range(8)], axis=0)
    y_s = np.concatenate([outs[i]["y"][2048:].reshape(NB, 8, D) for i in range(8)], axis=0)
    ret_p = np.stack([outs[i]["retp"] for i in range(8)], axis=1)
    pool_p = np.stack([outs[i]["poolp"] for i in range(8)], axis=1)
    ret_s = np.concatenate([outs[i]["rets"] for i in range(8)], axis=1)
    pool_s = np.concatenate([outs[i]["pools"] for i in range(8)], axis=1)
    return (y_p.astype(np.float32), y_s.astype(np.float32), ret_p.astype(np.float32),
            pool_p.astype(np.float32), ret_s.astype(np.float32), pool_s.astype(np.float32))
```

```python
import numpy as np
from contextlib import ExitStack

import concourse.bass as bass
import concourse.mybir as mybir
from concourse.alu_op_type import AluOpType as ALU
from concourse.bass_utils import run_bass_kernel_spmd

F32 = mybir.dt.float32
BF = mybir.dt.bfloat16
AF = mybir.ActivationFunctionType

NL = 4
NT = 17
D = 1024
DIN = 3072
NB = 16
EPS = 1e-6
SMP_POS = 8
PUMP_ASAP = True
RING = 10
NJ = 8
WINDOWS = (2, 4, 8, 16)

KF_COS = 0
KF_SIN = KF_COS + 17 * 64
KF_DM = KF_SIN + 17 * 128
KF_QD = KF_DM + 1024
KF_KD = KF_QD + 1024
KF_ID = KF_KD + 8
KF_IND = KF_ID + 128
KF_E = KF_IND + 16
KF_N = KF_E + 128
KB_ID = 0
KB_BAND = 128
KB_IND = KB_BAND + 6 * 512
KB_N = KB_IND + 16


def _gammas():
    h = np.arange(4, dtype=np.float64)
    return 1.0 - np.exp2(-5.0 - h)


def make_consts():
    kf = np.zeros((128, KF_N), np.float32)
    kb = np.zeros((128, KB_N), np.float32)
    p = np.arange(128)
    half = 64
    inv = (1.0 / (np.float32(10000.0) ** (np.arange(half, dtype=np.float32) / np.float32(half)))).astype(np.float32)
    cos = np.zeros((128, 17, 64), np.float32)
    sin = np.zeros((128, 17, 2, 64), np.float32)
    for t in range(17):
        pos = (128 * t + p) if t < 16 else (16384 + (p % 8))
        ang = (pos.astype(np.float32)[:, None] * inv[None, :]).astype(np.float32)
        c = np.cos(ang.astype(np.float64)).astype(np.float32)
        s = np.sin(ang.astype(np.float64)).astype(np.float32)
        cos[:, t] = c
        sin[:, t, 0] = -s
        sin[:, t, 1] = s
    kf[:, KF_COS:KF_COS + 17 * 64] = cos.reshape(128, -1)
    kf[:, KF_SIN:KF_SIN + 17 * 128] = sin.reshape(128, -1)
    g = _gammas()
    scale = 128.0 ** -0.5
    dm = np.zeros((128, 2, 4, 128), np.float64)
    qd = np.zeros((128, 2, 4, 128), np.float64)
    kd = np.zeros((128, 2, 4), np.float64)
    i = np.arange(128)
    for h in range(4):
        diff = i[None, :] - i[:, None]
        dm[:, 0, h, :] = np.where(diff >= 0, scale * g[h] ** np.maximum(diff, 0), 0.0)
        sj = i[:, None] % 8
        si = i[None, :] % 8
        same = (i[:, None] // 8) == (i[None, :] // 8)
        dm[:, 1, h, :] = np.where(same & (si >= sj), scale * g[h] ** np.maximum(si - sj, 0), 0.0)
        qd[:, 0, h, :] = (g[h] ** (i + 1))[None, :]
        qd[:, 1, h, :] = (g[h] ** ((i % 8) + 1))[None, :]
        kd[:, 0, h] = scale * g[h] ** (127 - i)
        kd[:, 1, h] = scale * g[h] ** (7 - (i % 8))
    kf[:, KF_DM:KF_DM + 1024] = dm.reshape(128, -1)
    kf[:, KF_QD:KF_QD + 1024] = qd.reshape(128, -1)
    kf[:, KF_KD:KF_KD + 8] = kd.reshape(128, -1)
    kf[:, KF_ID:KF_ID + 128] = np.eye(128)
    kf[:, KF_IND:KF_IND + 16] = (p[:, None] // 8) == np.arange(16)[None, :]
    E = np.zeros((128, 128), np.float32)
    E[0, :] = 1.0
    for b in range(16):
        E[32 + b, 8 * b:8 * b + 8] = 1.0
    kf[:, KF_E:KF_E + 128] = E
    kb[:, KB_ID:KB_ID + 128] = np.eye(128)
    band = np.zeros((128, 6, 4, 128), np.float64)
    tp = np.arange(128)[:, None]
    tt = np.arange(128)[None, :]
    for gi, w in enumerate(WINDOWS):
        inwin = (tp <= tt) & (tp > tt - w)
        cnt = np.minimum(tt + 1, w)
        band[:, 0, gi, :] = np.where(inwin, 1.0 / cnt, 0.0) - (tp == tt)
        band[:, 1, gi, :] = np.where(inwin, 1.0 / w, 0.0) - (tp == tt)
        band[:, 2, gi, :] = np.where(tp > 128 + tt - w, 1.0 / w, 0.0)
        sp_, st_ = tp % 8, tt % 8
        same = (tp // 8) == (tt // 8)
        band[:, 3, gi, :] = np.where(same & (sp_ <= st_) & (sp_ > st_ - w), 1.0 / w, 0.0) - (tp == tt)
        for k in range(2):
            rb = tp // 15 + 8 * k
            rr = tp % 15
            ok = (tp < 120) & (rb == (tt // 8)) & (rr > 15 + st_ - w)
            band[:, 4 + k, gi, :] = np.where(ok, 1.0 / w, 0.0)
    kb[:, KB_BAND:KB_BAND + 6 * 512] = band.reshape(128, -1)
    kb[:, KB_IND:KB_IND + 16] = (p[:, None] // 8) == np.arange(16)[None, :]
    return kf, kb


class Trk:
    def __init__(self, nc, es):
        self.nc = nc
        self.es = es
        self.engs = {}
        for name, e in [("pe", nc.tensor), ("act", nc.scalar), ("dve", nc.vector),
                        ("pool", nc.gpsimd), ("sp", nc.sync)]:
            sem = es.enter_context(nc.semaphore("s_" + name))
            self.engs[name] = dict(e=e, sem=sem, cnt=0, seen={}, name=name)
        self.lastw = {}
        self.readers = {}
        self.dsems = {}
        self.nwaits = 0
        self.nops = {}
        self.step = None
        self.alt = None
        self.tags = {}

    def _wait(self, E, deps):
        for (sem, val) in deps:
            key = id(sem)
            if E["seen"].get(key, 0) < val:
                E["e"].wait_ge(sem, val)
                E["seen"][key] = val
                self.nwaits += 1

    def _deps(self, E, reads, writes):
        deps = []
        pe = E["name"] == "pe"
        for k in reads:
            w = self.lastw.get(k)
            if w is not None and not (pe and w[0] is E["sem"]):
                deps.append(w)
            if k in self.EXCL:
                for r in self.readers.get(k, ()):
                    if r[0] is not E["sem"]:
                        deps.append(r)
        for k in writes:
            for r in self.readers.get(k, ()):
                if pe and r[0] is E["sem"]:
                    continue
                deps.append(r)
            w = self.lastw.get(k)
            if w is not None and not (pe and w[0] is E["sem"]):
                deps.append(w)
        return deps

    def _record(self, tok, reads, writes):
        for k in reads:
            self.readers.setdefault(k, []).append(tok)
        for k in writes:
            self.lastw[k] = tok
            self.readers[k] = []

    EXCL = frozenset(["Y0", "Y1", "Z0", "Z1", "Z2", "TR", "PB", "QBK"])

    PERSTEP = frozenset(["qrot", "krot", "vbf0", "vbf1", "vdec0", "vdec1", "sg", "mixp", "mixr.0", "mixr.1", "mixr.2", "mixr.3",
                         "mixT", "qkT", "qdT", "PT", "mTb", "h0", "ubf0", "ubf1", "ubf2", "ubfs"] + ["hT%d.%d" % (i, c) for i in range(2) for c in range(8)])

    def _tagcheck(self, reads, writes):
        if self.step is None:
            return
        for k in reads:
            if k in self.PERSTEP and self.tags.get(k) is not None:
                assert self.tags[k] in (self.step, self.alt), ("stale read", k, self.tags[k], self.step, self.alt)
        for k in writes:
            if k in self.PERSTEP:
                self.tags[k] = self.step

    def op(self, eng, fn, reads=(), writes=()):
        E = self.engs[eng]
        self._tagcheck(reads, writes)
        self._wait(E, self._deps(E, reads, writes))
        ins = fn(E["e"])
        E["cnt"] += 1
        ins.then_inc(E["sem"], 1)
        self.nops[eng] = self.nops.get(eng, 0) + 1
        self._record((E["sem"], E["cnt"]), reads, writes)
        return ins

    def dma(self, eng, out, in_, slot, reads=(), writes=(), **kw):
        E = self.engs[eng]
        self._wait(E, self._deps(E, reads, writes))
        if slot not in self.dsems:
            self.dsems[slot] = [self.es.enter_context(self.nc.semaphore("d_" + slot)), 0]
        d = self.dsems[slot]
        ins = E["e"].dma_start(out=out, in_=in_, **kw)
        d[1] += 16
        ins.then_inc(d[0], 16)
        self._record((d[0], d[1]), reads, writes)
        return ins

    def finish(self, eng="sp"):
        E = self.engs[eng]
        for slot, d in self.dsems.items():
            if d[1] > 0:
                E["e"].wait_ge(d[0], d[1])


class _Stop(Exception):
    pass


def build_program(n_layers=NL, interleave=True, stop=0):
    nc = bass.Bass("TRN2", target_bir_lowering=False)
    dr = lambda n, s, k: nc.dram_tensor(n, s, F32, kind=k).ap()
    xin = dr("xin", [NT * 128, D], "ExternalInput")
    cc = dr("cc", [48, D], "ExternalInput")
    sret = dr("sret", [NL, NB, 4, 128, 128], "ExternalInput")
    spool = dr("spool", [NL, NB * 15, 512], "ExternalInput")
    w_ada = dr("w_ada", [NL, D, DIN], "ExternalInput")
    b_ada = dr("b_ada", [NL, DIN], "ExternalInput")
    g_pre = dr("g_pre", [NL, D], "ExternalInput")
    g_post = dr("g_post", [NL, D], "ExternalInput")
    w_in = dr("w_in", [NL, D, DIN], "ExternalInput")
    w_pool = dr("w_pool", [NL, 4, 128, 128], "ExternalInput")
    pscale = dr("pool_scale", [NL, 512], "ExternalInput")
    w_o = dr("w_o", [NL, D, D], "ExternalInput")
    kf_d = dr("kf", [128, KF_N], "ExternalInput")
    kb_d = dr("kb", [128, KB_N], "ExternalInput")
    y = dr("y", [NT * 128, D], "ExternalOutput")
    retp = dr("retp", [NL, 4, 128, 128], "ExternalOutput")
    poolp = dr("poolp", [NL, 15, 512], "ExternalOutput")
    rets = dr("rets", [NL, NB, 4, 128, 128], "ExternalOutput")
    pools = dr("pools", [NL, NB, 15, 512], "ExternalOutput")

    with ExitStack() as es:
        T = Trk(nc, es)
        sb = lambda n, s, d=F32: es.enter_context(nc.sbuf_tensor(n, s, d))
        ps = lambda n, s, d=F32: es.enter_context(nc.psum_tensor(n, s, d))

        Y = ps("Y", [128, 1024])
        Z = [ps("Z%d" % i, [128, 512]) for i in range(3)]
        TR = ps("TR", [128, 8, 128], BF)
        PB = ps("PB", [128, 512])
        QB_ = ps("QBK", [128, 512])
        Y0 = Y[:, 0:512]
        Y1 = Y[:, 512:1024]
        PBt = PB[:].bitcast(BF).rearrange("p (c i) -> p c i", c=8)

        kf = sb("kfs", [128, KF_N])
        kb = sb("kbs", [128, KB_N], BF)
        ring = [sb("ring%d" % i, [128, 8, 512], BF) for i in range(RING)]
        xb = [sb("xb%d" % i, [128, D]) for i in range(4)]
        silucT = sb("silucT", [128, 8, 48], BF)
        GT = [sb("GT%d" % i, [128, 8, 48]) for i in range(2)]
        ST = [sb("ST%d" % i, [128, 8, 48]) for i in range(2)]
        GPr = [sb("GProws%d" % i, [48, D]) for i in range(2)]
        GPp = sb("GPp", [128, D])
        wps = [sb("wps%d" % i, [128, 4, 128], BF) for i in range(2)]
        wpl = sb("wpl", [128, 4, 128])
        psc = sb("psc", [128, 512])
        S = sb("S", [128, 4, 128])
        Sbf = sb("Sbf", [128, 4, 128], BF)
        wst = sb("wst", [128, 8, 256], BF)
        bch = sb("bch", [48, 256])
        gch = sb("gch", [48, 256])
        m1 = sb("m1", [48, 256])
        m2 = sb("m2", [48, 256])
        h0 = sb("h0", [128, D], BF)
        hT = [sb("hT%d" % i, [128, 8, 128], BF) for i in range(2)]
        ubf = [sb("ubf%d" % i, [128, 512], BF) for i in range(3)]
        ubfs = sb("ubfs", [128, 512], BF)
        u32 = sb("u32", [128, 512])
        qrot = sb("qrot", [128, 512], BF)
        krot = sb("krot", [128, 512], BF)
        t2 = sb("t2", [128, 512])
        sqo = u32
        vbf2 = [sb("vbf%d" % i, [128, 512], BF) for i in range(2)]
        vdec2 = [sb("vdec%d" % i, [128, 512], BF) for i in range(2)]
        sg = sb("sg", [128, D])
        qkT = sb("qkT", [128, 8, 128], BF)
        qdT = sb("qdT", [128, 4, 128], BF)
        PT = sb("PT", [128, 4, 128], BF)
        mTb = sb("mTb", [128, 4, 128], BF)
        mix = sb("mix", [128, D], BF)
        mixT = sb("mixT", [128, 8, 128], BF)
        tmpA = sb("tmpA", [128, 128])
        stA = [sb("stA%d" % i, [128, 4]) for i in range(2)]
        stC = [sb("stC%d" % i, [128, 12]) for i in range(2)]
        stD = [sb("stD%d" % i, [128, 4]) for i in range(2)]
        mhalf = sb("mhalf", [128, 4])
        s0f = [sb("s0f%d" % i, [128, 4, 128]) for i in range(2)]
        s0b = [sb("s0b%d" % i, [128, 4, 128], BF) for i in range(2)]
        vblk = [sb("vblk%d" % i, [128, 4, 128], BF) for i in range(2)]
        QBs = [sb("QBs%d" % i, [128, 4, 128], BF) for i in range(4)]
        hist = sb("hist", [128, 2, 512], BF)
        wstf = wst[:].rearrange("p c n -> p (c n)").bitcast(F32)
        s0in = [(s0f[0][:], "s0f0"), (s0f[1][:], "s0f1"),
                (t2[:].rearrange("p (b e) -> p b e", b=4), "t2"),
                (u32[:].rearrange("p (b e) -> p b e", b=4), "u32")]
        s0out = [(wstf[:, 0:512].rearrange("p (b e) -> p b e", b=4), "s0f2"),
                 (wstf[:, 512:1024].rearrange("p (b e) -> p b e", b=4), "s0f3")]
        s0_state = {}

        kcos = kf[:, KF_COS:KF_COS + 17 * 64].rearrange("p (t d) -> p t d", t=17)
        ksin = kf[:, KF_SIN:KF_SIN + 17 * 128].rearrange("p (t a d) -> p t a d", t=17, a=2)
        kdm = kf[:, KF_DM:KF_DM + 1024].rearrange("p (s h i) -> p s h i", s=2, h=4)
        kqd = kf[:, KF_QD:KF_QD + 1024].rearrange("p (s h i) -> p s h i", s=2, h=4)
        kkd = kf[:, KF_KD:KF_KD + 8].rearrange("p (s h) -> p s h", s=2)
        identf = kf[:, KF_ID:KF_ID + 128]
        kind = kf[:, KF_IND:KF_IND + 16]
        kE = kf[:, KF_E:KF_E + 128]
        identb = kb[:, KB_ID:KB_ID + 128]
        kband = kb[:, KB_BAND:KB_BAND + 6 * 512].rearrange("p (v g t) -> p v g t", v=6, g=4)
        kindb = kb[:, KB_IND:KB_IND + 16]
        gam = _gammas()
        gC = [float(gam[h] ** 128) for h in range(4)]
        gC8 = [float(gam[h] ** 8) for h in range(4)]

        def rstd_ops(st, n, inv_n):
            T.op("dve", lambda e: e.tensor_scalar(out=st[:, n:2 * n], in0=st[:, 0:n], scalar1=inv_n, scalar2=EPS,
                                                  op0=ALU.mult, op1=ALU.add), reads=[st.name], writes=[st.name])
            T.op("pool", lambda e: e.tensor_tensor(out=st[:, 2 * n:3 * n], in0=st[:, n:2 * n], in1=mhalf[:, 0:n],
                                                   op=ALU.pow), reads=[st.name, "mhalf"], writes=[st.name])

        def ring_slot(l, j):
            return (NJ * l + j) % RING

        def load_wtile(l, j):
            slot = ring_slot(l, j)
            if j < 6:
                src = w_in[l].rearrange("(c p) n -> p c n", p=128)[:, :, 512 * j:512 * j + 512]
            else:
                src = w_o[l].rearrange("(c p) n -> p c n", p=128)[:, :, 512 * (j - 6):512 * (j - 6) + 512]
            T.dma("pool", ring[slot][:], src, "ring%d" % slot, writes=["ring%d" % slot])

        wl_state = {"next": 0}

        def pump_weights(max_new=2):
            if PUMP_ASAP:
                max_new = 99
            issued = 0
            while wl_state["next"] < n_layers * NJ and issued < max_new:
                n = wl_state["next"]
                if wl_state.get("limit") is not None and n >= wl_state["limit"]:
                    break
                if n >= RING and not wl_done.get(n - RING, False):
                    break
                load_wtile(n // NJ, n % NJ)
                wl_state["next"] += 1
                issued += 1

        wl_done = {}

        def mark_wdone(l, j):
            wl_done[NJ * l + j] = True

        def mod_load(l, q, wload=True):
            T.step = None
            r = q // 4
            o = (q % 4) * 256
            c0 = 256 * q
            if wload:
                T.dma("pool", wst[:], w_ada[l].rearrange("(c p) n -> p c n", p=128)[:, :, c0:c0 + 256], "wst",
                      writes=["wst", "s0f2", "s0f3"])
            T.dma("sp", bch[:], b_ada[l:l + 1, c0:c0 + 256].broadcast_to([48, 256]), "bch", writes=["bch"])
            if r == 1:
                T.dma("sp", gch[:], g_pre[l:l + 1, o:o + 256].broadcast_to([48, 256]), "gch", writes=["gch"])
            elif r == 2:
                T.dma("sp", gch[:], g_post[l:l + 1, o:o + 256].broadcast_to([48, 256]), "gch", writes=["gch"])

        def mod_mm(l, q, wsrc=None, wkeys=("wst", "s0f2", "s0f3")):
            T.step = None
            par = l % 2
            r = q // 4
            o = (q % 4) * 256
            psM = QB_[0:48, 0:256]
            if wsrc is None:
                wsrc = wst
            for k in range(8):
                T.op("pe", lambda e: e.matmul(psM, lhsT=silucT[:, k, :], rhs=wsrc[:, k, :], start=(k == 0),
                                              stop=(k == 7)), reads=["silucT"] + list(wkeys), writes=["QBK"])
            T.op("dve", lambda e: e.tensor_tensor(out=m1[:], in0=psM, in1=bch[:], op=ALU.add),
                 reads=["QBK", "bch"], writes=["m1"])
            if r == 2:
                T.op("dve", lambda e: e.tensor_tensor(out=GPr[par][:, o:o + 256], in0=m1[:], in1=gch[:], op=ALU.mult),
                     reads=["m1", "gch"], writes=[GPr[par].name])
            elif r == 1:
                T.op("dve", lambda e: e.scalar_tensor_tensor(out=m2[:], in0=m1[:], scalar=1.0, in1=gch[:],
                                                             op0=ALU.add, op1=ALU.mult),
                     reads=["m1", "gch"], writes=["m2"])
            else:
                T.op("dve", lambda e: e.tensor_copy(out=m2[:], in_=m1[:]), reads=["m1"], writes=["m2"])

        def mod_tr(l, q):
            T.step = None
            par = l % 2
            r = q // 4
            if r == 2:
                return
            psMT = QB_[:, 256:352].rearrange("p (a b) -> p a b", a=2)
            for a in range(2):
                T.op("pe", lambda e: e.transpose(out=psMT[:, a, :], in_=m2[:, a * 128:(a + 1) * 128],
                                                 identity=identf[0:48, 0:48]),
                     reads=["m2", "kf"], writes=["QBK"])
            dst = (ST if r == 0 else GT)[par]
            cb = (q % 4) * 2
            T.op("act", lambda e: e.activation(out=dst[:, cb:cb + 2, :], in_=psMT, func=AF.Copy),
                 reads=["QBK"], writes=[dst.name])

        def mod_chunk(l, q, load=True):
            if load:
                mod_load(l, q)
            mod_mm(l, q)
            mod_tr(l, q)

        def build_gp(l, sample):
            dst = sg if sample else GPp
            if not sample:
                T.step = None
            gpr = GPr[l % 2]
            for n in range(2):
                if sample:
                    lhsT, rhs = kE[32:48, :], gpr[32:48, n * 512:(n + 1) * 512]
                else:
                    lhsT, rhs = kE[0:1, :], gpr[0:1, n * 512:(n + 1) * 512]
                T.op("pe", lambda e: e.matmul(QB_[:], lhsT=lhsT, rhs=rhs, start=True, stop=True),
                     reads=["kf", gpr.name], writes=["QBK"])
                T.op("act", lambda e: e.activation(out=dst[:, n * 512:(n + 1) * 512], in_=QB_[:], func=AF.Copy),
                     reads=["QBK"], writes=[dst.name])

        def layer_setup(l):
            T.step = None
            par = l % 2
            T.dma("sp", wpl[:], w_pool[l].rearrange("g c d -> c g d"), "wpl", writes=["wpl"])
            T.dma("sp", psc[:], pscale[l:l + 1, :].broadcast_to([128, 512]), "psc", writes=["psc"])
            T.op("dve", lambda e: e.tensor_tensor(out=wps[par][:], in0=wpl[:],
                                                  in1=psc[:].rearrange("p (g d) -> p g d", g=4), op=ALU.mult),
                 reads=["wpl", "psc"], writes=[wps[par].name])

        def load_hist(l):
            v = spool[l].rearrange("(k r) c -> r k c", k=2)
            T.dma("pool", hist[0:120, :, :], v, "hist", writes=["hist"])
            srcv = spool[l].rearrange("(b r) c -> b r c", r=15)[:, 8:15, :]
            T.dma("sp", pools[l][:, 0:7, :], srcv, "dd")

        def lt(s):
            l, pos = divmod(s, NT)
            if pos == SMP_POS:
                return l, 16
            return l, (pos if pos < SMP_POS else pos - 1)

        def xsrc(l, t):
            return (xin if l == 0 else y)[t * 128:(t + 1) * 128, :]

        def load_x(s):
            l, t = lt(s)
            k = s % 4
            T.dma("sp", xb[k][:], xsrc(l, t), "xl%d" % k, reads=[("yh", t)] if l > 0 else [],
                  writes=[xb[k].name])

        def stage_A1(s):
            T.step = s
            x = xb[s % 4]
            st = stA[s % 2]
            T.op("act", lambda e: e.activation(out=h0[:], in_=x[:], func=AF.Square, accum_out=st[:, 0:1]),
                 reads=[x.name], writes=["h0", st.name])
            rstd_ops(st, 1, 1.0 / D)

        def stage_A2(s):
            T.step = s
            x = xb[s % 4]
            st = stA[s % 2]
            T.op("act", lambda e: e.activation(out=h0[:], in_=x[:], func=AF.Copy, scale=st[:, 2:3]),
                 reads=[x.name, st.name], writes=["h0"])

        def stage_A_elem(s):
            stage_A1(s)
            stage_A2(s)

        def stage_A_pe(s, do_pe=True, do_ev=True):
            T.step = s
            l, t = lt(s)
            par = l % 2
            hTs = hT[s % 2]
            hk = [hTs.name + ".%d" % c for c in range(8)]
            if do_pe:
                for c in range(8):
                    T.op("pe", lambda e: e.transpose(out=PBt[:, c, :], in_=h0[:, c * 128:(c + 1) * 128],
                                                     identity=identb), reads=["h0", "kb"], writes=["PB"])
            if not do_ev:
                return
            if t < 16:
                for c in range(8):
                    T.op("dve", lambda e: e.tensor_scalar(out=hTs[:, c, :], in0=PBt[:, c, :],
                                                          scalar1=GT[par][:, c, 0:1], scalar2=ST[par][:, c, 0:1],
                                                          op0=ALU.mult, op1=ALU.add),
                         reads=["PB", GT[par].name, ST[par].name], writes=[hk[c]])
            else:
                for c in range(8):
                    gb = GT[par][:, c, 32:48].unsqueeze(2).broadcast_to([128, 16, 8])
                    sbb = ST[par][:, c, 32:48].unsqueeze(2).broadcast_to([128, 16, 8])
                    tv = tmpA[:].rearrange("p (b s) -> p b s", b=16)
                    T.op("dve", lambda e: e.tensor_tensor(out=tv, in0=PBt[:, c, :].rearrange("p (b s) -> p b s", b=16),
                                                          in1=gb, op=ALU.mult),
                         reads=["PB", GT[par].name], writes=["tmpA"])
                    T.op("dve", lambda e: e.tensor_tensor(out=hTs[:, c, :].rearrange("p (b s) -> p b s", b=16),
                                                          in0=tv, in1=sbb, op=ALU.add),
                         reads=["tmpA", ST[par].name], writes=[hk[c]])

        def stage_A(s):
            stage_A_elem(s)
            stage_A_pe(s)

        def rope(zb, t, dst):
            qv = zb[:].rearrange("p (h a d) -> p h a d", h=4, a=2)
            t2v = t2[:].rearrange("p (h a d) -> p h a d", h=4, a=2)
            dv = dst[:].rearrange("p (h a d) -> p h a d", h=4, a=2)
            cosB = kcos[:, t, :].unsqueeze(1).unsqueeze(1).broadcast_to([128, 4, 2, 64])
            for a in range(2):
                sinB = ksin[:, t, a, :].unsqueeze(1).broadcast_to([128, 4, 64])
                T.op("dve", lambda e: e.tensor_tensor(out=t2v[:, :, a, :], in0=qv[:, :, 1 - a, :], in1=sinB,
                                                      op=ALU.mult), reads=[zb.name, "kf"], writes=["t2"])
            T.op("dve", lambda e: e.tensor_tensor(out=qv, in0=qv, in1=cosB, op=ALU.mult),
                 reads=[zb.name, "kf"], writes=[zb.name])
            T.op("dve", lambda e: e.tensor_tensor(out=dv, in0=qv, in1=t2v, op=ALU.add),
                 reads=[zb.name, "t2"], writes=[dst.name])

        def stage_B(s):
            l, t = lt(s)
            smp = 1 if t == 16 else 0
            hTs = hT[s % 2]
            vbf, vdec = vbf2[s % 2], vdec2[s % 2]
            zb = [Z[0], Z[1], Z[0], Z[2], Z[1], Z[2]]

            def pe(j):
                T.step = s
                slot = ring_slot(l, j)
                for k in range(8):
                    T.op("pe", lambda e: e.matmul(zb[j][:], lhsT=hTs[:, k, :], rhs=ring[slot][:, k, :],
                                                  start=(k == 0), stop=(k == 7)),
                         reads=[hTs.name + ".%d" % k, "ring%d" % slot], writes=[zb[j].name])
                if t == 15:
                    mark_wdone(l, j)
                    if j < 2 or PUMP_ASAP:
                        pump_weights(1)

            def ev(j):
                T.step = s
                if j == 0:
                    ub = ubfs if t == 16 else ubf[(16 * l + t) % 3]
                    T.op("act", lambda e: e.activation(out=ub[:], in_=Z[0][:], func=AF.Copy),
                         reads=["Z0"], writes=[ub.name])
                    if t >= 15:
                        T.op("act", lambda e: e.activation(out=u32[:], in_=Z[0][:], func=AF.Copy),
                             reads=["Z0"], writes=["u32"])
                        if t == 15:
                            T.dma("sp", poolp[l], u32[113:128, :], "u32o", reads=["u32"])
                        else:
                            T.dma("sp", pools[l][:, 7:15, :], u32[:], "u32o", reads=["u32"])
                elif j == 1:
                    rope(Z[1], t, qrot)
                elif j == 2:
                    rope(Z[0], t, krot)
                elif j == 3:
                    T.op("act", lambda e: e.activation(out=vbf[:], in_=Z[2][:], func=AF.Copy),
                         reads=["Z2"], writes=[vbf.name])
                    kdB = kkd[:, smp, :].unsqueeze(2).broadcast_to([128, 4, 128])
                    T.op("dve", lambda e: e.tensor_tensor(out=vdec[:].rearrange("p (h e) -> p h e", h=4),
                                                          in0=Z[2][:].rearrange("p (h e) -> p h e", h=4), in1=kdB,
                                                          op=ALU.mult), reads=["Z2", "kf"], writes=[vdec.name])
                elif j == 4:
                    T.op("act", lambda e: e.activation(out=sg[:, 0:512], in_=Z[1][:], func=AF.Silu),
                         reads=["Z1"], writes=["sg"])
                else:
                    T.op("act", lambda e: e.activation(out=sg[:, 512:1024], in_=Z[2][:], func=AF.Silu),
                         reads=["Z2"], writes=["sg"])

            ops = []
            for j in range(6):
                ops.append(lambda j=j: pe(j))
                ops.append(lambda j=j: ev(j))
            return ops

        def s0_load(l, p):
            h, bq = divmod(p, 4)
            buf, key = s0in[p % 4]
            T.dma("sp", buf, sret[l, 4 * bq:4 * bq + 4, h].rearrange("b d e -> d b e"), "s0l%d" % (p % 4),
                  writes=[key])

        def s0_prefetch(l, n=2):
            st_ = s0_state.setdefault(l, 0)
            for p in range(st_, n):
                s0_load(l, p)
            s0_state[l] = max(st_, n)

        smp_fill = {}

        def stage_C(s):
            T.step = s
            l, t = lt(s)
            par = l % 2
            smp = 1 if t == 16 else 0
            st = stC[s % 2]
            vbf, vdec = vbf2[s % 2], vdec2[s % 2]
            Qv = QB_[:].rearrange("p (h i) -> p h i", h=4)
            cross = (t > 0)
            for h in range(4):
                T.op("pe", lambda e: e.transpose(out=PBt[:, h, :], in_=qrot[:, h * 128:(h + 1) * 128], identity=identb),
                     reads=["qrot", "kb"], writes=["PB"])
            for h in range(4):
                T.op("pe", lambda e: e.transpose(out=PBt[:, 4 + h, :], in_=krot[:, h * 128:(h + 1) * 128],
                                                 identity=identb), reads=["krot", "kb"], writes=["PB"])
            T.op("act", lambda e: e.activation(out=qkT[:], in_=PBt, func=AF.Copy), reads=["PB"], writes=["qkT"])
            if cross:
                T.op("dve", lambda e: e.tensor_tensor(out=qdT[:], in0=PBt[:, 0:4, :], in1=kqd[:, smp], op=ALU.mult),
                     reads=["PB", "kf"], writes=["qdT"])
            yield
            T.step = s
            for h in range(4):
                T.op("pe", lambda e: e.matmul(Qv[:, h, :], lhsT=qkT[:, 4 + h, :], rhs=qkT[:, h, :], start=True,
                                              stop=True), reads=["qkT"], writes=["QBK"])
            T.op("dve", lambda e: e.tensor_tensor(out=PT[:], in0=Qv, in1=kdm[:, smp], op=ALU.mult),
                 reads=["QBK", "kf"], writes=["PT"])
            if not smp:
                for h in range(4):
                    hs = slice(h * 128, (h + 1) * 128)
                    T.op("pe", lambda e: e.matmul(Y1[:, hs], lhsT=krot[:, hs], rhs=vdec[:, hs], start=True, stop=True),
                         reads=["krot", vdec.name], writes=["Y1"])
            ucur = ubfs if t == 16 else ubf[(16 * l + t) % 3]
            uprev = ubf[(16 * l + t - 1) % 3]
            if 0 < t < 16:
                T.alt = l * NT + ((t - 1) if (t - 1) < SMP_POS else t)
            for g in range(4):
                gs = slice(g * 128, (g + 1) * 128)
                if t == 0:
                    T.op("pe", lambda e: e.matmul(PB[:, gs], lhsT=ucur[:, gs], rhs=kband[:, 0, g, :], start=True,
                                                  stop=True), reads=[ucur.name, "kb"], writes=["PB"])
                elif t < 16:
                    T.op("pe", lambda e: e.matmul(PB[:, gs], lhsT=ucur[:, gs], rhs=kband[:, 1, g, :], start=True,
                                                  stop=False), reads=[ucur.name, "kb"], writes=["PB"])
                    T.op("pe", lambda e: e.matmul(PB[:, gs], lhsT=uprev[:, gs], rhs=kband[:, 2, g, :], start=False,
                                                  stop=True), reads=[uprev.name, "kb"], writes=["PB"])
                else:
                    T.op("pe", lambda e: e.matmul(PB[:, gs], lhsT=ucur[:, gs], rhs=kband[:, 3, g, :], start=True,
                                                  stop=False), reads=[ucur.name, "kb"], writes=["PB"])
                    for k in range(2):
                        T.op("pe", lambda e: e.matmul(PB[:, gs], lhsT=hist[0:120, k, gs], rhs=kband[0:120, 4 + k, g, :],
                                                      start=False, stop=(k == 1)),
                             reads=["hist", "kb"], writes=["PB"])
            T.alt = None
            T.op("act", lambda e: e.activation(out=mTb[:].rearrange("p g t -> p (g t)"), in_=PB[:], func=AF.Copy),
                 reads=["PB"], writes=["mTb"])
            yield
            T.step = s
            YP, YPn = (Y1, "Y1") if smp else (Y0, "Y0")
            for g in range(4):
                gs = slice(g * 128, (g + 1) * 128)
                T.op("pe", lambda e: e.matmul(YP[:, gs], lhsT=mTb[:, g, :], rhs=wps[par][:, g, :], start=True,
                                              stop=True), reads=["mTb", wps[par].name], writes=[YPn])
            if not smp:
                for h in range(4):
                    hs = slice(h * 128, (h + 1) * 128)
                    T.op("pe", lambda e: e.matmul(Qv[:, h, :], lhsT=PT[:, h, :], rhs=vbf[:, hs], start=True,
                                                  stop=not cross), reads=["PT", vbf.name], writes=["QBK"])
                    if cross:
                        T.op("pe", lambda e: e.matmul(Qv[:, h, :], lhsT=qdT[:, h, :], rhs=Sbf[:, h, :], start=False,
                                                      stop=True), reads=["qdT", "Sbf"], writes=["QBK"])
            else:
                s0_prefetch(l, 4)

                def build_vblk(p_):
                    h_, bq_ = divmod(p_, 4)
                    vsrc = vdec[:, h_ * 128:(h_ + 1) * 128].unsqueeze(1).broadcast_to([128, 4, 128])
                    isrc = kindb[:, 4 * bq_:4 * bq_ + 4].unsqueeze(2).broadcast_to([128, 4, 128])
                    vb_ = vblk[p_ % 2]
                    T.op("dve", lambda e: e.tensor_tensor(out=vb_[:], in0=vsrc, in1=isrc, op=ALU.mult),
                         reads=[vdec.name, "kb"], writes=[vb_.name])

                for p in range(16):
                    h, bq = divmod(p, 4)
                    hs = slice(h * 128, (h + 1) * 128)
                    if p in smp_fill:
                        smp_fill.pop(p)()
                        T.step = s
                    if bq == 0:
                        T.op("pe", lambda e: e.matmul(Qv[:, h, :], lhsT=PT[:, h, :], rhs=vbf[:, hs], start=True,
                                                      stop=False), reads=["PT", vbf.name], writes=["QBK"])
                    k2 = p % 2
                    f, fk = s0in[p % 4]
                    o, ok = s0out[k2]
                    b_, vb = s0b[k2], vblk[k2]
                    T.op("act", lambda e: e.activation(out=b_[:], in_=f, func=AF.Copy), reads=[fk], writes=[b_.name])
                    qsrc = qdT[:, h, 32 * bq:32 * bq + 32].rearrange("p (b c) -> p b c", b=4)
                    qfl = QBs[bq][:].rearrange("p b i -> p (b i)")
                    qdst = bass.AP(tensor=qfl.tensor, offset=qfl.offset + 32 * bq,
                                   ap=[list(qfl.ap[0]), [136, 4], [1, 8]])
                    T.op("pool", lambda e: e.tensor_copy(out=qdst, in_=qsrc), reads=["qdT"],
                         writes=[QBs[bq].name])
                    for b in range(4):
                        T.op("pe", lambda e: e.matmul(Qv[:, h, :], lhsT=QBs[bq][:, b, :], rhs=b_[:, b, :],
                                                      start=False, stop=(bq == 3 and b == 3)),
                             reads=[QBs[bq].name, b_.name], writes=["QBK"])
                    if p == 0:
                        build_vblk(0)
                    if p + 1 < 16:
                        build_vblk(p + 1)
                    yb = Y0 if k2 == 0 else PB[:]
                    ybn = "Y0" if k2 == 0 else "PB"
                    T.op("pe", lambda e: e.matmul(yb, lhsT=krot[:, hs], rhs=vb[:].rearrange("p b e -> p (b e)"),
                                                  start=True, stop=True), reads=["krot", vb.name], writes=[ybn])
                    T.op("dve", lambda e: e.scalar_tensor_tensor(out=o.rearrange("p b e -> p (b e)"),
                                                                 in0=f.rearrange("p b e -> p (b e)"),
                                                                 scalar=gC8[h], in1=yb, op0=ALU.mult, op1=ALU.add),
                         reads=[fk, ybn], writes=[ok])
                    T.dma("sp", rets[l, 4 * bq:4 * bq + 4, h].rearrange("b d e -> d b e"), o, "s0o%d" % k2,
                          reads=[ok])
                    if p + 4 < 16:
                        s0_load(l, p + 4)
            yield
            T.step = s
            MK = ["mixp"] + ["mixr.%d" % h for h in range(4)]
            T.op("act", lambda e: e.activation(out=sqo[:], in_=QB_[:], func=AF.Square), reads=["QBK"], writes=["u32"])
            T.op("dve", lambda e: e.tensor_reduce(out=st[:, 0:4], in_=sqo[:].rearrange("p (h e) -> p h e", h=4),
                                                  axis=mybir.AxisListType.X, op=ALU.add),
                 reads=["u32"], writes=[st.name])
            rstd_ops(st, 4, 1.0 / 128)
            T.op("dve", lambda e: e.tensor_tensor(out=mix[:, 0:512], in0=YP, in1=sg[:, 0:512], op=ALU.mult),
                 reads=[YPn, "sg"], writes=["mixp"])
            for h in range(4):
                cs = slice(512 + h * 128, 512 + (h + 1) * 128)
                T.op("dve", lambda e: e.scalar_tensor_tensor(out=mix[:, cs], in0=Qv[:, h, :], scalar=st[:, 8 + h:9 + h],
                                                             in1=sg[:, cs], op0=ALU.mult, op1=ALU.mult),
                     reads=["QBK", st.name, "sg"], writes=[MK[1 + h]])
            yield
            T.step = s
            if not smp:
                SK = ["S.%d" % h for h in range(4)]
                if t == 0:
                    T.op("dve", lambda e: e.tensor_copy(out=S[:].rearrange("p h e -> p (h e)"), in_=Y1),
                         reads=["Y1"], writes=SK)
                else:
                    for h in range(4):
                        hs = slice(h * 128, (h + 1) * 128)
                        T.op("dve", lambda e: e.scalar_tensor_tensor(out=S[:, h, :], in0=S[:, h, :], scalar=gC[h],
                                                                     in1=Y1[:, hs], op0=ALU.mult, op1=ALU.add),
                             reads=[SK[h], "Y1"], writes=[SK[h]])
                if t == 15:
                    T.dma("sp", retp[l].rearrange("h d e -> d h e"), S[:], "So", reads=SK)
            yield
            T.step = s
            for c in range(8):
                T.op("pe", lambda e: e.transpose(out=TR[:, c, :], in_=mix[:, c * 128:(c + 1) * 128], identity=identb),
                     reads=[MK[0] if c < 4 else MK[c - 3], "kb"], writes=["TR"])
            T.op("act", lambda e: e.activation(out=mixT[:], in_=TR[:], func=AF.Copy), reads=["TR"], writes=["mixT"])
            yield
            T.step = s

        def sbf_cast(s):
            l, t = lt(s)
            if t < 15:
                T.step = None
                T.op("pool", lambda e: e.tensor_copy(out=Sbf[:], in_=S[:]), reads=["S.%d" % h for h in range(4)],
                     writes=["Sbf"])

        def stage_D(s):
            T.step = s
            l, t = lt(s)
            x = xb[s % 4]
            st = stD[s % 2]
            if t == 16:
                build_gp(l, True)
            gp = sg if t == 16 else GPp
            for n in range(2):
                slot = ring_slot(l, 6 + n)
                yn = Y0 if n == 0 else Y1
                for k in range(8):
                    T.op("pe", lambda e: e.matmul(yn, lhsT=mixT[:, k, :], rhs=ring[slot][:, k, :], start=(k == 0),
                                                  stop=(k == 7)), reads=["mixT", "ring%d" % slot],
                         writes=["Y%d" % n])
                if t == 15:
                    mark_wdone(l, 6 + n)
                    if PUMP_ASAP:
                        pump_weights()
            T.op("act", lambda e: e.activation(out=mix[:], in_=Y[:], func=AF.Square, accum_out=st[:, 0:1]),
                 reads=["Y0", "Y1"], writes=["mixp", "mixr.0", "mixr.1", "mixr.2", "mixr.3", st.name])
            T.op("dve", lambda e: e.tensor_tensor(out=Y[:], in0=Y[:], in1=gp[:], op=ALU.mult),
                 reads=["Y0", "Y1", gp.name], writes=["Y0", "Y1"])
            rstd_ops(st, 1, 1.0 / D)
            T.op("dve", lambda e: e.scalar_tensor_tensor(out=x[:], in0=Y[:], scalar=st[:, 2:3], in1=x[:],
                                                         op0=ALU.mult, op1=ALU.add),
                 reads=["Y0", "Y1", st.name, x.name], writes=[x.name])
            T.dma("sp", y[t * 128:(t + 1) * 128, :], x[:], "xs%d" % (s % 4), reads=[x.name], writes=[("yh", t)])

        def ck(n):
            if stop == n:
                raise _Stop()

        load_pos = [1, 2, 3, 4, 5, 9, 10, 11, 12, 13, 14, 15]
        MOD_LOAD_AT = {p_: q_ for q_, p_ in enumerate(load_pos)}
        MOD_MM_AT = {p_ + 1: q_ for q_, p_ in enumerate(load_pos)}
        MOD_TR_AT = {p_ + 2: q_ for q_, p_ in enumerate(load_pos) if q_ < 8}

        try:
            T.dma("sp", kf[:], kf_d, "kf", writes=["kf"])
            for c0 in range(0, KB_N, 1600):
                c1 = min(KB_N, c0 + 1600)
                T.dma("pool", kb[:, c0:c1], kb_d[:, c0:c1], "kb", writes=["kb"])
            T.op("dve", lambda e: e.memset(mhalf[:], -0.5), writes=["mhalf"])
            for i in range(4):
                T.op("pool", lambda e: e.memset(QBs[i][:], 0.0), writes=[QBs[i].name])
            T.op("pool", lambda e: e.memset(hist[:], 0.0), writes=["hist"])
            ck(1)
            T.dma("sp", sg[0:48, :], cc, "cc", writes=["sg"])
            T.op("act", lambda e: e.activation(out=xb[3][0:48, :], in_=sg[0:48, :], func=AF.Silu),
                 reads=["sg"], writes=[xb[3].name])
            cT = QB_[:, 0:384].rearrange("p (c r) -> p c r", c=8)
            for c in range(8):
                T.op("pe", lambda e: e.transpose(out=cT[:, c, :], in_=xb[3][0:48, c * 128:(c + 1) * 128],
                                                 identity=identf[0:48, 0:48]), reads=[xb[3].name, "kf"], writes=["QBK"])
            T.op("act", lambda e: e.activation(out=silucT[:], in_=cT, func=AF.Copy), reads=["QBK"], writes=["silucT"])
            ck(2)
            nsteps = n_layers * NT
            load_x(0)
            load_x(1)
            wl_state["limit"] = 8
            w_ada0 = w_ada[0].rearrange("(c p) n -> p c n", p=128)
            for qq in range(6):
                slot = 8 + qq % 2
                T.dma("pool", ring[slot][:], w_ada0[:, :, 512 * qq:512 * qq + 512], "ring%d" % slot,
                      writes=["ring%d" % slot])
                if qq == 1:
                    pump_weights(99)
                if qq >= 1:
                    for half in range(2):
                        q = 2 * (qq - 1) + half
                        mod_load(0, q, wload=False)
                        mod_mm(0, q, wsrc=ring[8 + (qq - 1) % 2][:, :, 256 * half:256 * half + 256],
                               wkeys=("ring%d" % (8 + (qq - 1) % 2),))
                        mod_tr(0, q)
            for half in range(2):
                q = 10 + half
                mod_load(0, q, wload=False)
                mod_mm(0, q, wsrc=ring[9][:, :, 256 * half:256 * half + 256], wkeys=("ring9",))
                mod_tr(0, q)
            wl_state["limit"] = None
            ck(3)
            pump_weights(99)
            layer_setup(0)
            load_hist(0)
            load_x(2)
            build_gp(0, False)
            ck(4)
            stage_A(0)
            if nsteps > 1:
                stage_A(1)
            ck(5)
            for f_ in stage_B(0):
                f_()
            ck(6)

            for s in range(nsteps):
                l, t = lt(s)
                pos = s % NT
                if pos == 0 and l > 0:
                    build_gp(l, False)
                gC_ = stage_C(s)
                Bops = stage_B(s + 1) if s + 1 < nsteps else [lambda: None] * 12
                U_PE, U_EV, Q_PE, Q_EV, K_PE, K_EV, V_PE, V_EV, G0_PE, G0_EV, G1_PE, G1_EV = Bops
                ncs = lambda: next(gC_)
                hasA = s + 2 < nsteps
                if interleave:
                    if s + 3 < nsteps:
                        load_x(s + 3)
                    if t == 16:
                        s0_prefetch(l)
                    ncs()
                    if hasA:
                        stage_A1(s + 2)
                    U_PE(); U_EV(); Q_PE(); Q_EV()
                    ncs()
                    if hasA:
                        stage_A2(s + 2)
                    pump_weights(1)
                    if t == 16:
                        smp_fill.clear()
                        smp_fill.update({2: K_PE, 6: lambda: (V_PE(), V_EV()), 10: G0_PE, 14: G1_PE})
                        ncs()
                        if hasA:
                            stage_A_pe(s + 2, do_ev=False)
                        ncs()
                        ncs()
                        if hasA:
                            stage_A_pe(s + 2, do_pe=False)
                        K_EV()
                        ncs()
                        stage_D(s)
                        G0_EV(); G1_EV()
                    else:
                        K_PE()
                        ncs()
                        if hasA:
                            stage_A_pe(s + 2, do_ev=False)
                        V_PE()
                        ncs()
                        V_EV()
                        ncs()
                        sbf_cast(s)
                        if hasA:
                            stage_A_pe(s + 2, do_pe=False)
                        K_EV()
                        G0_PE(); G0_EV()
                        ncs()
                        pump_weights(1)
                        if l + 1 < n_layers and pos in MOD_TR_AT:
                            mod_tr(l + 1, MOD_TR_AT[pos])
                        G1_PE(); G1_EV()
                        stage_D(s)
                else:
                    for ci, _ in enumerate(gC_):
                        ck(1000 + 10 * s + ci)
                    ck(1900 + s)
                    stage_D(s)
                    ck(2000 + s)
                    for f_ in Bops:
                        f_()
                    sbf_cast(s)
                    ck(3000 + s)
                pump_weights()
                if not interleave and s + 3 < nsteps:
                    load_x(s + 3)
                if not interleave and s + 2 < nsteps:
                    stage_A(s + 2)
                if l + 1 < n_layers:
                    if pos in MOD_TR_AT and not interleave:
                        mod_tr(l + 1, MOD_TR_AT[pos])
                    if pos in MOD_MM_AT:
                        mod_mm(l + 1, MOD_MM_AT[pos])
                    if pos in MOD_LOAD_AT:
                        mod_load(l + 1, MOD_LOAD_AT[pos])
                    if pos == 14:
                        layer_setup(l + 1)
                    if pos == 15:
                        load_hist(l + 1)
                if SMP_POS > 0 and pos == SMP_POS - 1:
                    s0_prefetch(l)
                if SMP_POS == 0 and pos == NT - 1 and l + 1 < n_layers:
                    s0_prefetch(l + 1)
                ck(4000 + s)
        except _Stop:
            pass
        T.finish()
        build_program.stats = dict(nops=dict(T.nops), nwaits=T.nwaits)
    return nc


_CACHE = {}


def kernel(x_prompt, x_sample, c_prompt, c_sample, state_ret, state_pool, w_ada, b_ada,
           g_pre, g_post, w_in, w_pool, pool_scale, w_o):
    f = lambda a: np.ascontiguousarray(np.asarray(a, dtype=np.float32))
    x_prompt, x_sample, c_prompt, c_sample = f(x_prompt), f(x_sample), f(c_prompt), f(c_sample)
    state_ret, state_pool = f(state_ret), f(state_pool)
    shared = dict(w_ada=f(w_ada), b_ada=f(b_ada), g_pre=f(g_pre), g_post=f(g_post), w_in=f(w_in),
                  w_pool=f(w_pool), pool_scale=f(pool_scale), w_o=f(w_o))
    kf, kb = make_consts()
    shared["kf"] = kf
    shared["kb"] = kb
    if "nc" not in _CACHE:
        _CACHE["nc"] = build_program()
    nc = _CACHE["nc"]
    in_maps = []
    for i in range(8):
        xs = x_sample[NB * i:NB * (i + 1)].reshape(NB * 8, D)
        cc = np.zeros((48, D), np.float32)
        cc[0] = c_prompt[i]
        cc[32:48] = c_sample[NB * i:NB * (i + 1)]
        m = dict(shared)
        m["xin"] = np.ascontiguousarray(np.concatenate([x_prompt[i], xs], axis=0))
        m["cc"] = cc
        m["sret"] = np.ascontiguousarray(state_ret[:, NB * i:NB * (i + 1)])
        m["spool"] = np.ascontiguousarray(state_pool[:, NB * i:NB * (i + 1)].reshape(NL, NB * 15, 512))
        in_maps.append(m)
    res = run_bass_kernel_spmd(nc, in_maps, core_ids=list(range(8)))
    outs = res.results
    y_p = np.stack([outs[i]["y"][:2048] for i in range(8)], axis=0)
    y_s = np.concatenate([outs[i]["y"][2048:].reshape(NB, 8, D) for i in range(8)], axis=0)
    ret_p = np.stack([outs[i]["retp"] for i in range(8)], axis=1)
    pool_p = np.stack([outs[i]["poolp"] for i in range(8)], axis=1)
    ret_s = np.concatenate([outs[i]["rets"] for i in range(8)], axis=1)
    pool_s = np.concatenate([outs[i]["pools"] for i in range(8)], axis=1)
    return (y_p.astype(np.float32), y_s.astype(np.float32), ret_p.astype(np.float32),
            pool_p.astype(np.float32), ret_s.astype(np.float32), pool_s.astype(np.float32))
```

```python
import numpy as np
from contextlib import ExitStack

import concourse.bass as bass
import concourse.mybir as mybir
from concourse.alu_op_type import AluOpType as ALU
from concourse.bass_utils import run_bass_kernel_spmd

F32 = mybir.dt.float32
BF = mybir.dt.bfloat16
AF = mybir.ActivationFunctionType

NL = 4
NT = 17
D = 1024
DIN = 3072
NB = 16
EPS = 1e-6
SMP_POS = 8
PUMP_ASAP = True
RING = 10
NJ = 8
WINDOWS = (2, 4, 8, 16)

KF_COS = 0
KF_SIN = KF_COS + 17 * 64
KF_DM = KF_SIN + 17 * 128
KF_QD = KF_DM + 1024
KF_KD = KF_QD + 1024
KF_ID = KF_KD + 8
KF_IND = KF_ID + 128
KF_E = KF_IND + 16
KF_N = KF_E + 128
KB_ID = 0
KB_BAND = 128
KB_IND = KB_BAND + 6 * 512
KB_N = KB_IND + 16


def _gammas():
    h = np.arange(4, dtype=np.float64)
    return 1.0 - np.exp2(-5.0 - h)


def make_consts():
    kf = np.zeros((128, KF_N), np.float32)
    kb = np.zeros((128, KB_N), np.float32)
    p = np.arange(128)
    half = 64
    inv = (1.0 / (np.float32(10000.0) ** (np.arange(half, dtype=np.float32) / np.float32(half)))).astype(np.float32)
    cos = np.zeros((128, 17, 64), np.float32)
    sin = np.zeros((128, 17, 2, 64), np.float32)
    for t in range(17):
        pos = (128 * t + p) if t < 16 else (16384 + (p % 8))
        ang = (pos.astype(np.float32)[:, None] * inv[None, :]).astype(np.float32)
        c = np.cos(ang.astype(np.float64)).astype(np.float32)
        s = np.sin(ang.astype(np.float64)).astype(np.float32)
        cos[:, t] = c
        sin[:, t, 0] = -s
        sin[:, t, 1] = s
    kf[:, KF_COS:KF_COS + 17 * 64] = cos.reshape(128, -1)
    kf[:, KF_SIN:KF_SIN + 17 * 128] = sin.reshape(128, -1)
    g = _gammas()
    scale = 128.0 ** -0.5
    dm = np.zeros((128, 2, 4, 128), np.float64)
    qd = np.zeros((128, 2, 4, 128), np.float64)
    kd = np.zeros((128, 2, 4), np.float64)
    i = np.arange(128)
    for h in range(4):
        diff = i[None, :] - i[:, None]
        dm[:, 0, h, :] = np.where(diff >= 0, scale * g[h] ** np.maximum(diff, 0), 0.0)
        sj = i[:, None] % 8
        si = i[None, :] % 8
        same = (i[:, None] // 8) == (i[None, :] // 8)
        dm[:, 1, h, :] = np.where(same & (si >= sj), scale * g[h] ** np.maximum(si - sj, 0), 0.0)
        qd[:, 0, h, :] = (g[h] ** (i + 1))[None, :]
        qd[:, 1, h, :] = (g[h] ** ((i % 8) + 1))[None, :]
        kd[:, 0, h] = scale * g[h] ** (127 - i)
        kd[:, 1, h] = scale * g[h] ** (7 - (i % 8))
    kf[:, KF_DM:KF_DM + 1024] = dm.reshape(128, -1)
    kf[:, KF_QD:KF_QD + 1024] = qd.reshape(128, -1)
    kf[:, KF_KD:KF_KD + 8] = kd.reshape(128, -1)
    kf[:, KF_ID:KF_ID + 128] = np.eye(128)
    kf[:, KF_IND:KF_IND + 16] = (p[:, None] // 8) == np.arange(16)[None, :]
    E = np.zeros((128, 128), np.float32)
    E[0, :] = 1.0
    for b in range(16):
        E[32 + b, 8 * b:8 * b + 8] = 1.0
    kf[:, KF_E:KF_E + 128] = E
    kb[:, KB_ID:KB_ID + 128] = np.eye(128)
    band = np.zeros((128, 6, 4, 128), np.float64)
    tp = np.arange(128)[:, None]
    tt = np.arange(128)[None, :]
    for gi, w in enumerate(WINDOWS):
        inwin = (tp <= tt) & (tp > tt - w)
        cnt = np.minimum(tt + 1, w)
        band[:, 0, gi, :] = np.where(inwin, 1.0 / cnt, 0.0) - (tp == tt)
        band[:, 1, gi, :] = np.where(inwin, 1.0 / w, 0.0) - (tp == tt)
        band[:, 2, gi, :] = np.where(tp > 128 + tt - w, 1.0 / w, 0.0)
        sp_, st_ = tp % 8, tt % 8
        same = (tp // 8) == (tt // 8)
        band[:, 3, gi, :] = np.where(same & (sp_ <= st_) & (sp_ > st_ - w), 1.0 / w, 0.0) - (tp == tt)
        for k in range(2):
            rb = tp // 15 + 8 * k
            rr = tp % 15
            ok = (tp < 120) & (rb == (tt // 8)) & (rr > 15 + st_ - w)
            band[:, 4 + k, gi, :] = np.where(ok, 1.0 / w, 0.0)
    kb[:, KB_BAND:KB_BAND + 6 * 512] = band.reshape(128, -1)
    kb[:, KB_IND:KB_IND + 16] = (p[:, None] // 8) == np.arange(16)[None, :]
    return kf, kb


class Trk:
    def __init__(self, nc, es):
        self.nc = nc
        self.es = es
        self.engs = {}
        for name, e in [("pe", nc.tensor), ("act", nc.scalar), ("dve", nc.vector),
                        ("pool", nc.gpsimd), ("sp", nc.sync)]:
            sem = es.enter_context(nc.semaphore("s_" + name))
            self.engs[name] = dict(e=e, sem=sem, cnt=0, seen={}, name=name)
        self.lastw = {}
        self.readers = {}
        self.dsems = {}
        self.nwaits = 0
        self.nops = {}
        self.step = None
        self.alt = None
        self.tags = {}

    def _wait(self, E, deps):
        for (sem, val) in deps:
            key = id(sem)
            if E["seen"].get(key, 0) < val:
                E["e"].wait_ge(sem, val)
                E["seen"][key] = val
                self.nwaits += 1

    def _deps(self, E, reads, writes):
        deps = []
        pe = E["name"] == "pe"
        for k in reads:
            w = self.lastw.get(k)
            if w is not None and not (pe and w[0] is E["sem"]):
                deps.append(w)
            if k in self.EXCL:
                for r in self.readers.get(k, ()):
                    if r[0] is not E["sem"]:
                        deps.append(r)
        for k in writes:
            for r in self.readers.get(k, ()):
                if pe and r[0] is E["sem"]:
                    continue
                deps.append(r)
            w = self.lastw.get(k)
            if w is not None and not (pe and w[0] is E["sem"]):
                deps.append(w)
        return deps

    def _record(self, tok, reads, writes):
        for k in reads:
            self.readers.setdefault(k, []).append(tok)
        for k in writes:
            self.lastw[k] = tok
            self.readers[k] = []

    EXCL = frozenset(["Y0", "Y1", "Z0", "Z1", "Z2", "TR", "PB", "QBK"])

    PERSTEP = frozenset(["qrot", "krot", "vbf0", "vbf1", "vdec0", "vdec1", "sg", "mixp", "mixr.0", "mixr.1", "mixr.2", "mixr.3",
                         "mixT", "qkT", "qdT", "PT", "mTb", "h0", "ubf0", "ubf1", "ubf2", "ubfs"] + ["hT%d.%d" % (i, c) for i in range(2) for c in range(8)])

    def _tagcheck(self, reads, writes):
        if self.step is None:
            return
        for k in reads:
            if k in self.PERSTEP and self.tags.get(k) is not None:
                assert self.tags[k] in (self.step, self.alt), ("stale read", k, self.tags[k], self.step, self.alt)
        for k in writes:
            if k in self.PERSTEP:
                self.tags[k] = self.step

    def op(self, eng, fn, reads=(), writes=()):
        E = self.engs[eng]
        self._tagcheck(reads, writes)
        self._wait(E, self._deps(E, reads, writes))
        ins = fn(E["e"])
        E["cnt"] += 1
        ins.then_inc(E["sem"], 1)
        self.nops[eng] = self.nops.get(eng, 0) + 1
        self._record((E["sem"], E["cnt"]), reads, writes)
        return ins

    def dma(self, eng, out, in_, slot, reads=(), writes=(), **kw):
        E = self.engs[eng]
        self._wait(E, self._deps(E, reads, writes))
        if slot not in self.dsems:
            self.dsems[slot] = [self.es.enter_context(self.nc.semaphore("d_" + slot)), 0]
        d = self.dsems[slot]
        ins = E["e"].dma_start(out=out, in_=in_, **kw)
        d[1] += 16
        ins.then_inc(d[0], 16)
        self._record((d[0], d[1]), reads, writes)
        return ins

    def finish(self, eng="sp"):
        E = self.engs[eng]
        for slot, d in self.dsems.items():
            if d[1] > 0:
                E["e"].wait_ge(d[0], d[1])


class _Stop(Exception):
    pass


def build_program(n_layers=NL, interleave=True, stop=0):
    nc = bass.Bass("TRN2", target_bir_lowering=False)
    dr = lambda n, s, k: nc.dram_tensor(n, s, F32, kind=k).ap()
    xin = dr("xin", [NT * 128, D], "ExternalInput")
    cc = dr("cc", [48, D], "ExternalInput")
    sret = dr("sret", [NL, NB, 4, 128, 128], "ExternalInput")
    spool = dr("spool", [NL, NB * 15, 512], "ExternalInput")
    w_ada = dr("w_ada", [NL, D, DIN], "ExternalInput")
    b_ada = dr("b_ada", [NL, DIN], "ExternalInput")
    g_pre = dr("g_pre", [NL, D], "ExternalInput")
    g_post = dr("g_post", [NL, D], "ExternalInput")
    w_in = dr("w_in", [NL, D, DIN], "ExternalInput")
    w_pool = dr("w_pool", [NL, 4, 128, 128], "ExternalInput")
    pscale = dr("pool_scale", [NL, 512], "ExternalInput")
    w_o = dr("w_o", [NL, D, D], "ExternalInput")
    kf_d = dr("kf", [128, KF_N], "ExternalInput")
    kb_d = dr("kb", [128, KB_N], "ExternalInput")
    y = dr("y", [NT * 128, D], "ExternalOutput")
    retp = dr("retp", [NL, 4, 128, 128], "ExternalOutput")
    poolp = dr("poolp", [NL, 15, 512], "ExternalOutput")
    rets = dr("rets", [NL, NB, 4, 128, 128], "ExternalOutput")
    pools = dr("pools", [NL, NB, 15, 512], "ExternalOutput")

    with ExitStack() as es:
        T = Trk(nc, es)
        sb = lambda n, s, d=F32: es.enter_context(nc.sbuf_tensor(n, s, d))
        ps = lambda n, s, d=F32: es.enter_context(nc.psum_tensor(n, s, d))

        Y = ps("Y", [128, 1024])
        Z = [ps("Z%d" % i, [128, 512]) for i in range(3)]
        TR = ps("TR", [128, 8, 128], BF)
        PB = ps("PB", [128, 512])
        QB_ = ps("QBK", [128, 512])
        Y0 = Y[:, 0:512]
        Y1 = Y[:, 512:1024]
        PBt = PB[:].bitcast(BF).rearrange("p (c i) -> p c i", c=8)

        kf = sb("kfs", [128, KF_N])
        kb = sb("kbs", [128, KB_N], BF)
        ring = [sb("ring%d" % i, [128, 8, 512], BF) for i in range(RING)]
        xb = [sb("xb%d" % i, [128, D]) for i in range(4)]
        silucT = sb("silucT", [128, 8, 48], BF)
        GT = [sb("GT%d" % i, [128, 8, 48]) for i in range(2)]
        ST = [sb("ST%d" % i, [128, 8, 48]) for i in range(2)]
        GPr = [sb("GProws%d" % i, [48, D]) for i in range(2)]
        GPp = sb("GPp", [128, D])
        wps = [sb("wps%d" % i, [128, 4, 128], BF) for i in range(2)]
        wpl = sb("wpl", [128, 4, 128])
        psc = sb("psc", [128, 512])
        S = sb("S", [128, 4, 128])
        Sbf = sb("Sbf", [128, 4, 128], BF)
        wst = sb("wst", [128, 8, 256], BF)
        bch = sb("bch", [48, 256])
        gch = sb("gch", [48, 256])
        m1 = sb("m1", [48, 256])
        m2 = sb("m2", [48, 256])
        h0 = sb("h0", [128, D], BF)
        hT = [sb("hT%d" % i, [128, 8, 128], BF) for i in range(2)]
        ubf = [sb("ubf%d" % i, [128, 512], BF) for i in range(3)]
        ubfs = sb("ubfs", [128, 512], BF)
        u32 = sb("u32", [128, 512])
        qrot = sb("qrot", [128, 512], BF)
        krot = sb("krot", [128, 512], BF)
        t2 = sb("t2", [128, 512])
        sqo = u32
        vbf2 = [sb("vbf%d" % i, [128, 512], BF) for i in range(2)]
        vdec2 = [sb("vdec%d" % i, [128, 512], BF) for i in range(2)]
        sg = sb("sg", [128, D])
        qkT = sb("qkT", [128, 8, 128], BF)
        qdT = sb("qdT", [128, 4, 128], BF)
        PT = sb("PT", [128, 4, 128], BF)
        mTb = sb("mTb", [128, 4, 128], BF)
        mix = sb("mix", [128, D], BF)
        mixT = sb("mixT", [128, 8, 128], BF)
        tmpA = sb("tmpA", [128, 128])
        stA = [sb("stA%d" % i, [128, 4]) for i in range(2)]
        stC = [sb("stC%d" % i, [128, 12]) for i in range(2)]
        stD = [sb("stD%d" % i, [128, 4]) for i in range(2)]
        mhalf = sb("mhalf", [128, 4])
        s0f = [sb("s0f%d" % i, [128, 4, 128]) for i in range(2)]
        s0b = [sb("s0b%d" % i, [128, 4, 128], BF) for i in range(2)]
        vblk = [sb("vblk%d" % i, [128, 4, 128], BF) for i in range(2)]
        QBs = [sb("QBs%d" % i, [128, 4, 128], BF) for i in range(4)]
        hist = sb("hist", [128, 2, 512], BF)
        wstf = wst[:].rearrange("p c n -> p (c n)").bitcast(F32)
        s0in = [(s0f[0][:], "s0f0"), (s0f[1][:], "s0f1"),
                (t2[:].rearrange("p (b e) -> p b e", b=4), "t2"),
                (u32[:].rearrange("p (b e) -> p b e", b=4), "u32")]
        s0out = [(wstf[:, 0:512].rearrange("p (b e) -> p b e", b=4), "s0f2"),
                 (wstf[:, 512:1024].rearrange("p (b e) -> p b e", b=4), "s0f3")]
        s0_state = {}

        kcos = kf[:, KF_COS:KF_COS + 17 * 64].rearrange("p (t d) -> p t d", t=17)
        ksin = kf[:, KF_SIN:KF_SIN + 17 * 128].rearrange("p (t a d) -> p t a d", t=17, a=2)
        kdm = kf[:, KF_DM:KF_DM + 1024].rearrange("p (s h i) -> p s h i", s=2, h=4)
        kqd = kf[:, KF_QD:KF_QD + 1024].rearrange("p (s h i) -> p s h i", s=2, h=4)
        kkd = kf[:, KF_KD:KF_KD + 8].rearrange("p (s h) -> p s h", s=2)
        identf = kf[:, KF_ID:KF_ID + 128]
        kind = kf[:, KF_IND:KF_IND + 16]
        kE = kf[:, KF_E:KF_E + 128]
        identb = kb[:, KB_ID:KB_ID + 128]
        kband = kb[:, KB_BAND:KB_BAND + 6 * 512].rearrange("p (v g t) -> p v g t", v=6, g=4)
        kindb = kb[:, KB_IND:KB_IND + 16]
        gam = _gammas()
        gC = [float(gam[h] ** 128) for h in range(4)]
        gC8 = [float(gam[h] ** 8) for h in range(4)]

        def rstd_ops(st, n, inv_n):
            T.op("dve", lambda e: e.tensor_scalar(out=st[:, n:2 * n], in0=st[:, 0:n], scalar1=inv_n, scalar2=EPS,
                                                  op0=ALU.mult, op1=ALU.add), reads=[st.name], writes=[st.name])
            T.op("pool", lambda e: e.tensor_tensor(out=st[:, 2 * n:3 * n], in0=st[:, n:2 * n], in1=mhalf[:, 0:n],
                                                   op=ALU.pow), reads=[st.name, "mhalf"], writes=[st.name])

        def ring_slot(l, j):
            return (NJ * l + j) % RING

        def load_wtile(l, j):
            slot = ring_slot(l, j)
            if j < 6:
                src = w_in[l].rearrange("(c p) n -> p c n", p=128)[:, :, 512 * j:512 * j + 512]
            else:
                src = w_o[l].rearrange("(c p) n -> p c n", p=128)[:, :, 512 * (j - 6):512 * (j - 6) + 512]
            T.dma("pool", ring[slot][:], src, "ring%d" % slot, writes=["ring%d" % slot])

        wl_state = {"next": 0}

        def pump_weights(max_new=2):
            if PUMP_ASAP:
                max_new = 99
            issued = 0
            while wl_state["next"] < n_layers * NJ and issued < max_new:
                n = wl_state["next"]
                if wl_state.get("limit") is not None and n >= wl_state["limit"]:
                    break
                if n >= RING and not wl_done.get(n - RING, False):
                    break
                load_wtile(n // NJ, n % NJ)
                wl_state["next"] += 1
                issued += 1

        wl_done = {}

        def mark_wdone(l, j):
            wl_done[NJ * l + j] = True

        def mod_load(l, q, wload=True):
            T.step = None
            r = q // 4
            o = (q % 4) * 256
            c0 = 256 * q
            if wload:
                T.dma("pool", wst[:], w_ada[l].rearrange("(c p) n -> p c n", p=128)[:, :, c0:c0 + 256], "wst",
                      writes=["wst", "s0f2", "s0f3"])
            T.dma("sp", bch[:], b_ada[l:l + 1, c0:c0 + 256].broadcast_to([48, 256]), "bch", writes=["bch"])
            if r == 1:
                T.dma("sp", gch[:], g_pre[l:l + 1, o:o + 256].broadcast_to([48, 256]), "gch", writes=["gch"])
            elif r == 2:
                T.dma("sp", gch[:], g_post[l:l + 1, o:o + 256].broadcast_to([48, 256]), "gch", writes=["gch"])

        def mod_mm(l, q, wsrc=None, wkeys=("wst", "s0f2", "s0f3")):
            T.step = None
            par = l % 2
            r = q // 4
            o = (q % 4) * 256
            psM = QB_[0:48, 0:256]
            if wsrc is None:
                wsrc = wst
            for k in range(8):
                T.op("pe", lambda e: e.matmul(psM, lhsT=silucT[:, k, :], rhs=wsrc[:, k, :], start=(k == 0),
                                              stop=(k == 7)), reads=["silucT"] + list(wkeys), writes=["QBK"])
            T.op("dve", lambda e: e.tensor_tensor(out=m1[:], in0=psM, in1=bch[:], op=ALU.add),
                 reads=["QBK", "bch"], writes=["m1"])
            if r == 2:
                T.op("dve", lambda e: e.tensor_tensor(out=GPr[par][:, o:o + 256], in0=m1[:], in1=gch[:], op=ALU.mult),
                     reads=["m1", "gch"], writes=[GPr[par].name])
            elif r == 1:
                T.op("dve", lambda e: e.scalar_tensor_tensor(out=m2[:], in0=m1[:], scalar=1.0, in1=gch[:],
                                                             op0=ALU.add, op1=ALU.mult),
                     reads=["m1", "gch"], writes=["m2"])
            else:
                T.op("dve", lambda e: e.tensor_copy(out=m2[:], in_=m1[:]), reads=["m1"], writes=["m2"])

        def mod_tr(l, q):
            T.step = None
            par = l % 2
            r = q // 4
            if r == 2:
                return
            psMT = QB_[:, 256:352].rearrange("p (a b) -> p a b", a=2)
            for a in range(2):
                T.op("pe", lambda e: e.transpose(out=psMT[:, a, :], in_=m2[:, a * 128:(a + 1) * 128],
                                                 identity=identf[0:48, 0:48]),
                     reads=["m2", "kf"], writes=["QBK"])
            dst = (ST if r == 0 else GT)[par]
            cb = (q % 4) * 2
            T.op("act", lambda e: e.activation(out=dst[:, cb:cb + 2, :], in_=psMT, func=AF.Copy),
                 reads=["QBK"], writes=[dst.name])

        def mod_chunk(l, q, load=True):
            if load:
                mod_load(l, q)
            mod_mm(l, q)
            mod_tr(l, q)

        def build_gp(l, sample):
            dst = sg if sample else GPp
            if not sample:
                T.step = None
            gpr = GPr[l % 2]
            for n in range(2):
                if sample:
                    lhsT, rhs = kE[32:48, :], gpr[32:48, n * 512:(n + 1) * 512]
                else:
                    lhsT, rhs = kE[0:1, :], gpr[0:1, n * 512:(n + 1) * 512]
                T.op("pe", lambda e: e.matmul(QB_[:], lhsT=lhsT, rhs=rhs, start=True, stop=True),
                     reads=["kf", gpr.name], writes=["QBK"])
                T.op("act", lambda e: e.activation(out=dst[:, n * 512:(n + 1) * 512], in_=QB_[:], func=AF.Copy),
                     reads=["QBK"], writes=[dst.name])

        def layer_setup(l):
            T.step = None
            par = l % 2
            T.dma("sp", wpl[:], w_pool[l].rearrange("g c d -> c g d"), "wpl", writes=["wpl"])
            T.dma("sp", psc[:], pscale[l:l + 1, :].broadcast_to([128, 512]), "psc", writes=["psc"])
            T.op("dve", lambda e: e.tensor_tensor(out=wps[par][:], in0=wpl[:],
                                                  in1=psc[:].rearrange("p (g d) -> p g d", g=4), op=ALU.mult),
                 reads=["wpl", "psc"], writes=[wps[par].name])

        def load_hist(l):
            v = spool[l].rearrange("(k r) c -> r k c", k=2)
            T.dma("pool", hist[0:120, :, :], v, "hist", writes=["hist"])
            srcv = spool[l].rearrange("(b r) c -> b r c", r=15)[:, 8:15, :]
            T.dma("sp", pools[l][:, 0:7, :], srcv, "dd")

        def lt(s):
            l, pos = divmod(s, NT)
            if pos == SMP_POS:
                return l, 16
            return l, (pos if pos < SMP_POS else pos - 1)

        def xsrc(l, t):
            return (xin if l == 0 else y)[t * 128:(t + 1) * 128, :]

        def load_x(s):
            l, t = lt(s)
            k = s % 4
            T.dma("sp", xb[k][:], xsrc(l, t), "xl%d" % k, reads=[("yh", t)] if l > 0 else [],
                  writes=[xb[k].name])

        def stage_A1(s):
            T.step = s
            x = xb[s % 4]
            st = stA[s % 2]
            T.op("act", lambda e: e.activation(out=h0[:], in_=x[:], func=AF.Square, accum_out=st[:, 0:1]),
                 reads=[x.name], writes=["h0", st.name])
            rstd_ops(st, 1, 1.0 / D)

        def stage_A2(s):
            T.step = s
            x = xb[s % 4]
            st = stA[s % 2]
            T.op("act", lambda e: e.activation(out=h0[:], in_=x[:], func=AF.Copy, scale=st[:, 2:3]),
                 reads=[x.name, st.name], writes=["h0"])

        def stage_A_elem(s):
            stage_A1(s)
            stage_A2(s)

        def stage_A_pe(s, do_pe=True, do_ev=True):
            T.step = s
            l, t = lt(s)
            par = l % 2
            hTs = hT[s % 2]
            hk = [hTs.name + ".%d" % c for c in range(8)]
            if do_pe:
                for c in range(8):
                    T.op("pe", lambda e: e.transpose(out=PBt[:, c, :], in_=h0[:, c * 128:(c + 1) * 128],
                                                     identity=identb), reads=["h0", "kb"], writes=["PB"])
            if not do_ev:
                return
            if t < 16:
                for c in range(8):
                    T.op("dve", lambda e: e.tensor_scalar(out=hTs[:, c, :], in0=PBt[:, c, :],
                                                          scalar1=GT[par][:, c, 0:1], scalar2=ST[par][:, c, 0:1],
                                                          op0=ALU.mult, op1=ALU.add),
                         reads=["PB", GT[par].name, ST[par].name], writes=[hk[c]])
            else:
                for c in range(8):
                    gb = GT[par][:, c, 32:48].unsqueeze(2).broadcast_to([128, 16, 8])
                    sbb = ST[par][:, c, 32:48].unsqueeze(2).broadcast_to([128, 16, 8])
                    tv = tmpA[:].rearrange("p (b s) -> p b s", b=16)
                    T.op("dve", lambda e: e.tensor_tensor(out=tv, in0=PBt[:, c, :].rearrange("p (b s) -> p b s", b=16),
                                                          in1=gb, op=ALU.mult),
                         reads=["PB", GT[par].name], writes=["tmpA"])
                    T.op("dve", lambda e: e.tensor_tensor(out=hTs[:, c, :].rearrange("p (b s) -> p b s", b=16),
                                                          in0=tv, in1=sbb, op=ALU.add),
                         reads=["tmpA", ST[par].name], writes=[hk[c]])

        def stage_A(s):
            stage_A_elem(s)
            stage_A_pe(s)

        def rope(zb, t, dst):
            qv = zb[:].rearrange("p (h a d) -> p h a d", h=4, a=2)
            t2v = t2[:].rearrange("p (h a d) -> p h a d", h=4, a=2)
            dv = dst[:].rearrange("p (h a d) -> p h a d", h=4, a=2)
            cosB = kcos[:, t, :].unsqueeze(1).unsqueeze(1).broadcast_to([128, 4, 2, 64])
            for a in range(2):
                sinB = ksin[:, t, a, :].unsqueeze(1).broadcast_to([128, 4, 64])
                T.op("dve", lambda e: e.tensor_tensor(out=t2v[:, :, a, :], in0=qv[:, :, 1 - a, :], in1=sinB,
                                                      op=ALU.mult), reads=[zb.name, "kf"], writes=["t2"])
            T.op("dve", lambda e: e.tensor_tensor(out=qv, in0=qv, in1=cosB, op=ALU.mult),
                 reads=[zb.name, "kf"], writes=[zb.name])
            T.op("dve", lambda e: e.tensor_tensor(out=dv, in0=qv, in1=t2v, op=ALU.add),
                 reads=[zb.name, "t2"], writes=[dst.name])

        def stage_B(s):
            l, t = lt(s)
            smp = 1 if t == 16 else 0
            hTs = hT[s % 2]
            vbf, vdec = vbf2[s % 2], vdec2[s % 2]
            zb = [Z[0], Z[1], Z[0], Z[2], Z[1], Z[2]]

            def pe(j):
                T.step = s
                slot = ring_slot(l, j)
                for k in range(8):
                    T.op("pe", lambda e: e.matmul(zb[j][:], lhsT=hTs[:, k, :], rhs=ring[slot][:, k, :],
                                                  start=(k == 0), stop=(k == 7)),
                         reads=[hTs.name + ".%d" % k, "ring%d" % slot], writes=[zb[j].name])
                if t == 15:
                    mark_wdone(l, j)
                    if j < 2 or PUMP_ASAP:
                        pump_weights(1)

            def ev(j):
                T.step = s
                if j == 0:
                    ub = ubfs if t == 16 else ubf[(16 * l + t) % 3]
                    T.op("act", lambda e: e.activation(out=ub[:], in_=Z[0][:], func=AF.Copy),
                         reads=["Z0"], writes=[ub.name])
                    if t >= 15:
                        T.op("act", lambda e: e.activation(out=u32[:], in_=Z[0][:], func=AF.Copy),
                             reads=["Z0"], writes=["u32"])
                        if t == 15:
                            T.dma("sp", poolp[l], u32[113:128, :], "u32o", reads=["u32"])
                        else:
                            T.dma("sp", pools[l][:, 7:15, :], u32[:], "u32o", reads=["u32"])
                elif j == 1:
                    rope(Z[1], t, qrot)
                elif j == 2:
                    rope(Z[0], t, krot)
                elif j == 3:
                    T.op("act", lambda e: e.activation(out=vbf[:], in_=Z[2][:], func=AF.Copy),
                         reads=["Z2"], writes=[vbf.name])
                    kdB = kkd[:, smp, :].unsqueeze(2).broadcast_to([128, 4, 128])
                    T.op("dve", lambda e: e.tensor_tensor(out=vdec[:].rearrange("p (h e) -> p h e", h=4),
                                                          in0=Z[2][:].rearrange("p (h e) -> p h e", h=4), in1=kdB,
                                                          op=ALU.mult), reads=["Z2", "kf"], writes=[vdec.name])
                elif j == 4:
                    T.op("act", lambda e: e.activation(out=sg[:, 0:512], in_=Z[1][:], func=AF.Silu),
                         reads=["Z1"], writes=["sg"])
                else:
                    T.op("act", lambda e: e.activation(out=sg[:, 512:1024], in_=Z[2][:], func=AF.Silu),
                         reads=["Z2"], writes=["sg"])

            ops = []
            for j in range(6):
                ops.append(lambda j=j: pe(j))
                ops.append(lambda j=j: ev(j))
            return ops

        def s0_load(l, p):
            h, bq = divmod(p, 4)
            buf, key = s0in[p % 4]
            T.dma("sp", buf, sret[l, 4 * bq:4 * bq + 4, h].rearrange("b d e -> d b e"), "s0l%d" % (p % 4),
                  writes=[key])

        def s0_prefetch(l, n=2):
            st_ = s0_state.setdefault(l, 0)
            for p in range(st_, n):
                s0_load(l, p)
            s0_state[l] = max(st_, n)

        smp_fill = {}

        def stage_C(s):
            T.step = s
            l, t = lt(s)
            par = l % 2
            smp = 1 if t == 16 else 0
            st = stC[s % 2]
            vbf, vdec = vbf2[s % 2], vdec2[s % 2]
            Qv = QB_[:].rearrange("p (h i) -> p h i", h=4)
            cross = (t > 0)
            for h in range(4):
                T.op("pe", lambda e: e.transpose(out=PBt[:, h, :], in_=qrot[:, h * 128:(h + 1) * 128], identity=identb),
                     reads=["qrot", "kb"], writes=["PB"])
            for h in range(4):
                T.op("pe", lambda e: e.transpose(out=PBt[:, 4 + h, :], in_=krot[:, h * 128:(h + 1) * 128],
                                                 identity=identb), reads=["krot", "kb"], writes=["PB"])
            T.op("act", lambda e: e.activation(out=qkT[:], in_=PBt, func=AF.Copy), reads=["PB"], writes=["qkT"])
            if cross:
                T.op("dve", lambda e: e.tensor_tensor(out=qdT[:], in0=PBt[:, 0:4, :], in1=kqd[:, smp], op=ALU.mult),
                     reads=["PB", "kf"], writes=["qdT"])
            yield
            T.step = s
            for h in range(4):
                T.op("pe", lambda e: e.matmul(Qv[:, h, :], lhsT=qkT[:, 4 + h, :], rhs=qkT[:, h, :], start=True,
                                              stop=True), reads=["qkT"], writes=["QBK"])
            T.op("dve", lambda e: e.tensor_tensor(out=PT[:], in0=Qv, in1=kdm[:, smp], op=ALU.mult),
                 reads=["QBK", "kf"], writes=["PT"])
            if not smp:
                for h in range(4):
                    hs = slice(h * 128, (h + 1) * 128)
                    T.op("pe", lambda e: e.matmul(Y1[:, hs], lhsT=krot[:, hs], rhs=vdec[:, hs], start=True, stop=True),
                         reads=["krot", vdec.name], writes=["Y1"])
            ucur = ubfs if t == 16 else ubf[(16 * l + t) % 3]
            uprev = ubf[(16 * l + t - 1) % 3]
            if 0 < t < 16:
                T.alt = l * NT + ((t - 1) if (t - 1) < SMP_POS else t)
            for g in range(4):
                gs = slice(g * 128, (g + 1) * 128)
                if t == 0:
                    T.op("pe", lambda e: e.matmul(PB[:, gs], lhsT=ucur[:, gs], rhs=kband[:, 0, g, :], start=True,
                                                  stop=True), reads=[ucur.name, "kb"], writes=["PB"])
                elif t < 16:
                    T.op("pe", lambda e: e.matmul(PB[:, gs], lhsT=ucur[:, gs], rhs=kband[:, 1, g, :], start=True,
                                                  stop=False), reads=[ucur.name, "kb"], writes=["PB"])
                    T.op("pe", lambda e: e.matmul(PB[:, gs], lhsT=uprev[:, gs], rhs=kband[:, 2, g, :], start=False,
                                                  stop=True), reads=[uprev.name, "kb"], writes=["PB"])
                else:
                    T.op("pe", lambda e: e.matmul(PB[:, gs], lhsT=ucur[:, gs], rhs=kband[:, 3, g, :], start=True,
                                                  stop=False), reads=[ucur.name, "kb"], writes=["PB"])
                    for k in range(2):
                        T.op("pe", lambda e: e.matmul(PB[:, gs], lhsT=hist[0:120, k, gs], rhs=kband[0:120, 4 + k, g, :],
                                                      start=False, stop=(k == 1)),
                             reads=["hist", "kb"], writes=["PB"])
            T.alt = None
            T.op("act", lambda e: e.activation(out=mTb[:].rearrange("p g t -> p (g t)"), in_=PB[:], func=AF.Copy),
                 reads=["PB"], writes=["mTb"])
            yield
            T.step = s
            YP, YPn = (Y1, "Y1") if smp else (Y0, "Y0")
            for g in range(4):
                gs = slice(g * 128, (g + 1) * 128)
                T.op("pe", lambda e: e.matmul(YP[:, gs], lhsT=mTb[:, g, :], rhs=wps[par][:, g, :], start=True,
                                              stop=True), reads=["mTb", wps[par].name], writes=[YPn])
            if not smp:
                for h in range(4):
                    hs = slice(h * 128, (h + 1) * 128)
                    T.op("pe", lambda e: e.matmul(Qv[:, h, :], lhsT=PT[:, h, :], rhs=vbf[:, hs], start=True,
                                                  stop=not cross), reads=["PT", vbf.name], writes=["QBK"])
                    if cross:
                        T.op("pe", lambda e: e.matmul(Qv[:, h, :], lhsT=qdT[:, h, :], rhs=Sbf[:, h, :], start=False,
                                                      stop=True), reads=["qdT", "Sbf"], writes=["QBK"])
            else:
                s0_prefetch(l, 4)

                def build_vblk(p_):
                    h_, bq_ = divmod(p_, 4)
                    vsrc = vdec[:, h_ * 128:(h_ + 1) * 128].unsqueeze(1).broadcast_to([128, 4, 128])
                    isrc = kindb[:, 4 * bq_:4 * bq_ + 4].unsqueeze(2).broadcast_to([128, 4, 128])
                    vb_ = vblk[p_ % 2]
                    T.op("dve", lambda e: e.tensor_tensor(out=vb_[:], in0=vsrc, in1=isrc, op=ALU.mult),
                         reads=[vdec.name, "kb"], writes=[vb_.name])

                for p in range(16):
                    h, bq = divmod(p, 4)
                    hs = slice(h * 128, (h + 1) * 128)
                    if p in smp_fill:
                        smp_fill.pop(p)()
                        T.step = s
                    if bq == 0:
                        T.op("pe", lambda e: e.matmul(Qv[:, h, :], lhsT=PT[:, h, :], rhs=vbf[:, hs], start=True,
                                                      stop=False), reads=["PT", vbf.name], writes=["QBK"])
                    k2 = p % 2
                    f, fk = s0in[p % 4]
                    o, ok = s0out[k2]
                    b_, vb = s0b[k2], vblk[k2]
                    T.op("act", lambda e: e.activation(out=b_[:], in_=f, func=AF.Copy), reads=[fk], writes=[b_.name])
                    qsrc = qdT[:, h, 32 * bq:32 * bq + 32].rearrange("p (b c) -> p b c", b=4)
                    qfl = QBs[bq][:].rearrange("p b i -> p (b i)")
                    qdst = bass.AP(tensor=qfl.tensor, offset=qfl.offset + 32 * bq,
                                   ap=[list(qfl.ap[0]), [136, 4], [1, 8]])
                    T.op("pool", lambda e: e.tensor_copy(out=qdst, in_=qsrc), reads=["qdT"],
                         writes=[QBs[bq].name])
                    for b in range(4):
                        T.op("pe", lambda e: e.matmul(Qv[:, h, :], lhsT=QBs[bq][:, b, :], rhs=b_[:, b, :],
                                                      start=False, stop=(bq == 3 and b == 3)),
                             reads=[QBs[bq].name, b_.name], writes=["QBK"])
                    if p == 0:
                        build_vblk(0)
                    if p + 1 < 16:
                        build_vblk(p + 1)
                    yb = Y0 if k2 == 0 else PB[:]
                    ybn = "Y0" if k2 == 0 else "PB"
                    T.op("pe", lambda e: e.matmul(yb, lhsT=krot[:, hs], rhs=vb[:].rearrange("p b e -> p (b e)"),
                                                  start=True, stop=True), reads=["krot", vb.name], writes=[ybn])
                    T.op("dve", lambda e: e.scalar_tensor_tensor(out=o.rearrange("p b e -> p (b e)"),
                                                                 in0=f.rearrange("p b e -> p (b e)"),
                                                                 scalar=gC8[h], in1=yb, op0=ALU.mult, op1=ALU.add),
                         reads=[fk, ybn], writes=[ok])
                    T.dma("sp", rets[l, 4 * bq:4 * bq + 4, h].rearrange("b d e -> d b e"), o, "s0o%d" % k2,
                          reads=[ok])
                    if p + 4 < 16:
                        s0_load(l, p + 4)
            yield
            T.step = s
            MK = ["mixp"] + ["mixr.%d" % h for h in range(4)]
            T.op("act", lambda e: e.activation(out=sqo[:], in_=QB_[:], func=AF.Square), reads=["QBK"], writes=["u32"])
            T.op("dve", lambda e: e.tensor_reduce(out=st[:, 0:4], in_=sqo[:].rearrange("p (h e) -> p h e", h=4),
                                                  axis=mybir.AxisListType.X, op=ALU.add),
                 reads=["u32"], writes=[st.name])
            rstd_ops(st, 4, 1.0 / 128)
            T.op("dve", lambda e: e.tensor_tensor(out=mix[:, 0:512], in0=YP, in1=sg[:, 0:512], op=ALU.mult),
                 reads=[YPn, "sg"], writes=["mixp"])
            for h in range(4):
                cs = slice(512 + h * 128, 512 + (h + 1) * 128)
                T.op("dve", lambda e: e.scalar_tensor_tensor(out=mix[:, cs], in0=Qv[:, h, :], scalar=st[:, 8 + h:9 + h],
                                                             in1=sg[:, cs], op0=ALU.mult, op1=ALU.mult),
                     reads=["QBK", st.name, "sg"], writes=[MK[1 + h]])
            yield
            T.step = s
            if not smp:
                SK = ["S.%d" % h for h in range(4)]
                if t == 0:
                    T.op("dve", lambda e: e.tensor_copy(out=S[:].rearrange("p h e -> p (h e)"), in_=Y1),
                         reads=["Y1"], writes=SK)
                else:
                    for h in range(4):
                        hs = slice(h * 128, (h + 1) * 128)
                        T.op("dve", lambda e: e.scalar_tensor_tensor(out=S[:, h, :], in0=S[:, h, :], scalar=gC[h],
                                                                     in1=Y1[:, hs], op0=ALU.mult, op1=ALU.add),
                             reads=[SK[h], "Y1"], writes=[SK[h]])
                if t == 15:
                    T.dma("sp", retp[l].rearrange("h d e -> d h e"), S[:], "So", reads=SK)
            yield
            T.step = s
            for c in range(8):
                T.op("pe", lambda e: e.transpose(out=TR[:, c, :], in_=mix[:, c * 128:(c + 1) * 128], identity=identb),
                     reads=[MK[0] if c < 4 else MK[c - 3], "kb"], writes=["TR"])
            T.op("act", lambda e: e.activation(out=mixT[:], in_=TR[:], func=AF.Copy), reads=["TR"], writes=["mixT"])
            yield
            T.step = s

        def sbf_cast(s):
            l, t = lt(s)
            if t < 15:
                T.step = None
                T.op("pool", lambda e: e.tensor_copy(out=Sbf[:], in_=S[:]), reads=["S.%d" % h for h in range(4)],
                     writes=["Sbf"])

        def stage_D(s):
            T.step = s
            l, t = lt(s)
            x = xb[s % 4]
            st = stD[s % 2]
            if t == 16:
                build_gp(l, True)
            gp = sg if t == 16 else GPp
            for n in range(2):
                slot = ring_slot(l, 6 + n)
                yn = Y0 if n == 0 else Y1
                for k in range(8):
                    T.op("pe", lambda e: e.matmul(yn, lhsT=mixT[:, k, :], rhs=ring[slot][:, k, :], start=(k == 0),
                                                  stop=(k == 7)), reads=["mixT", "ring%d" % slot],
                         writes=["Y%d" % n])
                if t == 15:
                    mark_wdone(l, 6 + n)
                    if PUMP_ASAP:
                        pump_weights()
            T.op("act", lambda e: e.activation(out=mix[:], in_=Y[:], func=AF.Square, accum_out=st[:, 0:1]),
                 reads=["Y0", "Y1"], writes=["mixp", "mixr.0", "mixr.1", "mixr.2", "mixr.3", st.name])
            T.op("dve", lambda e: e.tensor_tensor(out=Y[:], in0=Y[:], in1=gp[:], op=ALU.mult),
                 reads=["Y0", "Y1", gp.name], writes=["Y0", "Y1"])
            rstd_ops(st, 1, 1.0 / D)
            T.op("dve", lambda e: e.scalar_tensor_tensor(out=x[:], in0=Y[:], scalar=st[:, 2:3], in1=x[:],
                                                         op0=ALU.mult, op1=ALU.add),
                 reads=["Y0", "Y1", st.name, x.name], writes=[x.name])
            T.dma("sp", y[t * 128:(t + 1) * 128, :], x[:], "xs%d" % (s % 4), reads=[x.name], writes=[("yh", t)])

        def ck(n):
            if stop == n:
                raise _Stop()

        load_pos = [1, 2, 3, 4, 5, 9, 10, 11, 12, 13, 14, 15]
        MOD_LOAD_AT = {p_: q_ for q_, p_ in enumerate(load_pos)}
        MOD_MM_AT = {p_ + 1: q_ for q_, p_ in enumerate(load_pos)}
        MOD_TR_AT = {p_ + 2: q_ for q_, p_ in enumerate(load_pos) if q_ < 8}

        try:
            T.dma("sp", kf[:], kf_d, "kf", writes=["kf"])
            for c0 in range(0, KB_N, 1600):
                c1 = min(KB_N, c0 + 1600)
                T.dma("pool", kb[:, c0:c1], kb_d[:, c0:c1], "kb", writes=["kb"])
            T.op("dve", lambda e: e.memset(mhalf[:], -0.5), writes=["mhalf"])
            for i in range(4):
                T.op("pool", lambda e: e.memset(QBs[i][:], 0.0), writes=[QBs[i].name])
            T.op("pool", lambda e: e.memset(hist[:], 0.0), writes=["hist"])
            ck(1)
            T.dma("sp", sg[0:48, :], cc, "cc", writes=["sg"])
            T.op("act", lambda e: e.activation(out=xb[3][0:48, :], in_=sg[0:48, :], func=AF.Silu),
                 reads=["sg"], writes=[xb[3].name])
            cT = QB_[:, 0:384].rearrange("p (c r) -> p c r", c=8)
            for c in range(8):
                T.op("pe", lambda e: e.transpose(out=cT[:, c, :], in_=xb[3][0:48, c * 128:(c + 1) * 128],
                                                 identity=identf[0:48, 0:48]), reads=[xb[3].name, "kf"], writes=["QBK"])
            T.op("act", lambda e: e.activation(out=silucT[:], in_=cT, func=AF.Copy), reads=["QBK"], writes=["silucT"])
            ck(2)
            nsteps = n_layers * NT
            load_x(0)
            load_x(1)
            w_ada0 = w_ada[0].rearrange("(c p) n -> p c n", p=128)
            for qq in range(6):
                T.dma("pool", ring[4 + qq][:], w_ada0[:, :, 512 * qq:512 * qq + 512], "ring%d" % (4 + qq),
                      writes=["ring%d" % (4 + qq)])
            wl_state["limit"] = 4
            pump_weights(99)
            bch5, gch5 = xb[2][0:48, 0:512], xb[2][0:48, 512:1024]
            m15, m25 = xb[3][0:48, 0:512], xb[3][0:48, 512:1024]
            psM5 = QB_[0:48, :]
            psT5 = PB[:, 0:192].rearrange("p (a b) -> p a b", a=4)
            for qq in range(6):
                r = qq // 2
                o = (qq % 2) * 512
                rk = "ring%d" % (4 + qq)
                T.dma("sp", bch5, b_ada[0:1, 512 * qq:512 * qq + 512].broadcast_to([48, 512]), "bch", writes=[xb[2].name])
                if r > 0:
                    gsrc = g_pre if r == 1 else g_post
                    T.dma("sp", gch5, gsrc[0:1, o:o + 512].broadcast_to([48, 512]), "gch", writes=[xb[2].name])
                for k in range(8):
                    T.op("pe", lambda e: e.matmul(psM5, lhsT=silucT[:, k, :], rhs=ring[4 + qq][:, k, :], start=(k == 0),
                                                  stop=(k == 7)), reads=["silucT", rk], writes=["QBK"])
                T.op("dve", lambda e: e.tensor_tensor(out=m15, in0=psM5, in1=bch5, op=ALU.add),
                     reads=["QBK", xb[2].name], writes=[xb[3].name])
                if r == 2:
                    T.op("dve", lambda e: e.tensor_tensor(out=GPr[0][:, o:o + 512], in0=m15, in1=gch5, op=ALU.mult),
                         reads=[xb[3].name, xb[2].name], writes=[GPr[0].name])
                    continue
                src5 = m15
                if r == 1:
                    T.op("dve", lambda e: e.scalar_tensor_tensor(out=m25, in0=m15, scalar=1.0, in1=gch5,
                                                                 op0=ALU.add, op1=ALU.mult),
                         reads=[xb[3].name, xb[2].name], writes=[xb[3].name])
                    src5 = m25
                for a in range(4):
                    T.op("pe", lambda e: e.transpose(out=psT5[:, a, :], in_=src5[:, a * 128:(a + 1) * 128],
                                                     identity=identf[0:48, 0:48]),
                         reads=[xb[3].name, "kf"], writes=["PB"])
                dst5 = (ST if r == 0 else GT)[0]
                cb = (qq % 2) * 4
                T.op("act", lambda e: e.activation(out=dst5[:, cb:cb + 4, :], in_=psT5, func=AF.Copy),
                     reads=["PB"], writes=[dst5.name])
            wl_state["limit"] = None
            ck(3)
            pump_weights(99)
            layer_setup(0)
            load_hist(0)
            load_x(2)
            build_gp(0, False)
            ck(4)
            stage_A(0)
            if nsteps > 1:
                stage_A(1)
            ck(5)
            for f_ in stage_B(0):
                f_()
            ck(6)

            for s in range(nsteps):
                l, t = lt(s)
                pos = s % NT
                if pos == 0 and l > 0:
                    build_gp(l, False)
                gC_ = stage_C(s)
                Bops = stage_B(s + 1) if s + 1 < nsteps else [lambda: None] * 12
                U_PE, U_EV, Q_PE, Q_EV, K_PE, K_EV, V_PE, V_EV, G0_PE, G0_EV, G1_PE, G1_EV = Bops
                ncs = lambda: next(gC_)
                hasA = s + 2 < nsteps
                if interleave:
                    if s + 3 < nsteps:
                        load_x(s + 3)
                    if t == 16:
                        s0_prefetch(l)
                    ncs()
                    if hasA:
                        stage_A1(s + 2)
                    U_PE(); U_EV(); Q_PE(); Q_EV()
                    ncs()
                    if hasA:
                        stage_A2(s + 2)
                    pump_weights(1)
                    if t == 16:
                        smp_fill.clear()
                        smp_fill.update({2: K_PE, 6: lambda: (V_PE(), V_EV()), 10: G0_PE, 14: G1_PE})
                        ncs()
                        if hasA:
                            stage_A_pe(s + 2, do_ev=False)
                        ncs()
                        ncs()
                        if hasA:
                            stage_A_pe(s + 2, do_pe=False)
                        K_EV()
                        ncs()
                        stage_D(s)
                        G0_EV(); G1_EV()
                    else:
                        K_PE()
                        ncs()
                        if hasA:
                            stage_A_pe(s + 2, do_ev=False)
                        V_PE()
                        ncs()
                        V_EV()
                        ncs()
                        sbf_cast(s)
                        if hasA:
                            stage_A_pe(s + 2, do_pe=False)
                        K_EV()
                        G0_PE(); G0_EV()
                        ncs()
                        pump_weights(1)
                        if l + 1 < n_layers and pos in MOD_TR_AT:
                            mod_tr(l + 1, MOD_TR_AT[pos])
                        G1_PE(); G1_EV()
                        stage_D(s)
                else:
                    for ci, _ in enumerate(gC_):
                        ck(1000 + 10 * s + ci)
                    ck(1900 + s)
                    stage_D(s)
                    ck(2000 + s)
                    for f_ in Bops:
                        f_()
                    sbf_cast(s)
                    ck(3000 + s)
                pump_weights()
                if not interleave and s + 3 < nsteps:
                    load_x(s + 3)
                if not interleave and s + 2 < nsteps:
                    stage_A(s + 2)
                if l + 1 < n_layers:
                    if pos in MOD_TR_AT and not interleave:
                        mod_tr(l + 1, MOD_TR_AT[pos])
                    if pos in MOD_MM_AT:
                        mod_mm(l + 1, MOD_MM_AT[pos])
                    if pos in MOD_LOAD_AT:
                        mod_load(l + 1, MOD_LOAD_AT[pos])
                    if pos == 14:
                        layer_setup(l + 1)
                    if pos == 15:
                        load_hist(l + 1)
                if SMP_POS > 0 and pos == SMP_POS - 1:
                    s0_prefetch(l)
                if SMP_POS == 0 and pos == NT - 1 and l + 1 < n_layers:
                    s0_prefetch(l + 1)
                ck(4000 + s)
        except _Stop:
            pass
        T.finish()
        build_program.stats = dict(nops=dict(T.nops), nwaits=T.nwaits)
    return nc


_CACHE = {}


def kernel(x_prompt, x_sample, c_prompt, c_sample, state_ret, state_pool, w_ada, b_ada,
           g_pre, g_post, w_in, w_pool, pool_scale, w_o):
    f = lambda a: np.ascontiguousarray(np.asarray(a, dtype=np.float32))
    x_prompt, x_sample, c_prompt, c_sample = f(x_prompt), f(x_sample), f(c_prompt), f(c_sample)
    state_ret, state_pool = f(state_ret), f(state_pool)
    shared = dict(w_ada=f(w_ada), b_ada=f(b_ada), g_pre=f(g_pre), g_post=f(g_post), w_in=f(w_in),
                  w_pool=f(w_pool), pool_scale=f(pool_scale), w_o=f(w_o))
    kf, kb = make_consts()
    shared["kf"] = kf
    shared["kb"] = kb
    if "nc" not in _CACHE:
        _CACHE["nc"] = build_program()
    nc = _CACHE["nc"]
    in_maps = []
    for i in range(8):
        xs = x_sample[NB * i:NB * (i + 1)].reshape(NB * 8, D)
        cc = np.zeros((48, D), np.float32)
        cc[0] = c_prompt[i]
        cc[32:48] = c_sample[NB * i:NB * (i + 1)]
        m = dict(shared)
        m["xin"] = np.ascontiguousarray(np.concatenate([x_prompt[i], xs], axis=0))
        m["cc"] = cc
        m["sret"] = np.ascontiguousarray(state_ret[:, NB * i:NB * (i + 1)])
        m["spool"] = np.ascontiguousarray(state_pool[:, NB * i:NB * (i + 1)].reshape(NL, NB * 15, 512))
        in_maps.append(m)
    res = run_bass_kernel_spmd(nc, in_maps, core_ids=list(range(8)))
    outs = res.results
    y_p = np.stack([outs[i]["y"][:2048] for i in range(8)], axis=0)
    y_s = np.concatenate([outs[i]["y"][2048:].reshape(NB, 8, D) for i in range(8)], axis=0)
    ret_p = np.stack([outs[i]["retp"] for i in range(8)], axis=1)
    pool_p = np.stack([outs[i]["poolp"] for i in range(8)], axis=1)
    ret_s = np.concatenate([outs[i]["rets"] for i in range(8)], axis=1)
    pool_s = np.concatenate([outs[i]["pools"] for i in range(8)], axis=1)
    return (y_p.astype(np.float32), y_s.astype(np.float32), ret_p.astype(np.float32),
            pool_p.astype(np.float32), ret_s.astype(np.float32), pool_s.astype(np.float32))
```
